# Optimizing a Trainium2 kernel written in Bass

```python
import math
import jax, jax.numpy as jnp
from jax import lax
import numpy as np

D_MODEL = 2048
BATCH = 4
SEQ = 4096
DEPTH = 4

N_EVEN = (DEPTH + 1) // 2
N_ODD = DEPTH // 2
DEEPNORM_ALPHA = (2 * DEPTH) ** 0.25
DEEPNORM_BETA = (8 * DEPTH) ** -0.25
MLA_HEADS = 16
Q_LORA = 1536
KV_LORA = 512
NOPE_DIM = 128
ROPE_DIM = 64
V_DIM = 128
ROPE_THETA = 10000.0
Q_BLOCK = 128
MLA_IN = Q_LORA + KV_LORA + ROPE_DIM
MLA_OUT = MLA_HEADS * V_DIM
RWKV_HEADS = 32
RWKV_HEAD_DIM = 64
RWKV_WIDTH = RWKV_HEADS * RWKV_HEAD_DIM
DECAY_LORA = 96
AAA_LORA = 96
GATE_LORA = 256
RWKV_IN = 3 * RWKV_WIDTH + 2 * DECAY_LORA + 2 * AAA_LORA + GATE_LORA
RWKV_DECAY_SCALE = 0.6065306597126334
RWKV_GN_EPS = 64e-5
EVEN_IN = MLA_IN + RWKV_IN
EVEN_MIX = MLA_OUT + RWKV_WIDTH
GDN_K_HEADS = 16
GDN_V_HEADS = 32
GDN_DK = 128
GDN_DV = 128
GDN_KEY_WIDTH = GDN_K_HEADS * GDN_DK
GDN_VAL_WIDTH = GDN_V_HEADS * GDN_DV
GDN_QKV = 2 * GDN_KEY_WIDTH + GDN_VAL_WIDTH
CONV_WIDTH = 5
CHUNK = 64
ODD_IN = GDN_QKV + GDN_VAL_WIDTH + 4 * GDN_V_HEADS
N_EXPERTS = 16
EXPERT_FF = 1024
CAPACITY_FACTOR = 2

kernel_name = 'hybrid_mla_rwkv7_gdn_ecmoe_encoder'


def layer_norm(x, g, b, eps=1e-5):
    xf = x.astype(jnp.float32)
    mu = jnp.mean(xf, -1, keepdims=True)
    var = jnp.mean(jnp.square(xf - mu), -1, keepdims=True)
    return ((xf - mu) * lax.rsqrt(var + eps) * g + b).astype(x.dtype)


def rms_norm(x, g, eps=1e-6):
    xf = x.astype(jnp.float32)
    return (xf * lax.rsqrt(jnp.mean(jnp.square(xf), -1, keepdims=True) + eps) * g).astype(x.dtype)


def l2_normalize(x, eps=1e-6):
    xf = x.astype(jnp.float32)
    return (xf * lax.rsqrt(jnp.sum(jnp.square(xf), -1, keepdims=True) + eps)).astype(x.dtype)


def rope_tables(positions):
    inv = ROPE_THETA ** (-jnp.arange(0, ROPE_DIM, 2, dtype=jnp.float32) / ROPE_DIM)
    ang = positions.astype(jnp.float32)[..., None] * inv
    return jnp.cos(ang), jnp.sin(ang)


def apply_rope(x, cos, sin):
    x1, x2 = jnp.split(x.astype(jnp.float32), 2, axis=-1)
    return jnp.concatenate([x1 * cos - x2 * sin, x1 * sin + x2 * cos], -1).astype(x.dtype)


def centred_shift(z):
    prev = jnp.pad(z[:, :-1], ((0, 0), (1, 0), (0, 0)))
    nxt = jnp.pad(z[:, 1:], ((0, 0), (0, 1), (0, 0)))
    return 0.5 * (prev + nxt)


def mla_mixer(p, positions, q_norm, w_uq, kv_norm, w_ukv):
    Bn, T, _ = p.shape
    c_q, c_kv, k_rope = jnp.split(p, [Q_LORA, Q_LORA + KV_LORA], axis=-1)
    q = (rms_norm(c_q, q_norm) @ w_uq).reshape(Bn, T, MLA_HEADS, NOPE_DIM + ROPE_DIM)
    kv = (rms_norm(c_kv, kv_norm) @ w_ukv).reshape(Bn, T, MLA_HEADS, NOPE_DIM + V_DIM)
    q_nope, q_rope = q[..., :NOPE_DIM], q[..., NOPE_DIM:]
    k_nope, v = kv[..., :NOPE_DIM], kv[..., NOPE_DIM:]
    cos, sin = rope_tables(positions)
    q_rope = apply_rope(q_rope, cos[:, :, None], sin[:, :, None])
    k_rope = apply_rope(k_rope, cos, sin)
    scale = (NOPE_DIM + ROPE_DIM) ** -0.5
    nb = T // Q_BLOCK

    def blocks(z):
        return jnp.moveaxis(z.reshape(Bn, nb, Q_BLOCK, *z.shape[2:]), 1, 0)

    def attend(qb):
        qn, qr = qb
        s = jnp.einsum('bqhd,bkhd->bhqk', qn, k_nope) + jnp.einsum('bqhd,bkd->bhqk', qr, k_rope)
        pr = jax.nn.softmax(s.astype(jnp.float32) * scale, axis=-1).astype(v.dtype)
        return jnp.einsum('bhqk,bkhd->bqhd', pr, v)

    o = lax.map(attend, (blocks(q_nope), blocks(q_rope)))
    return jnp.moveaxis(o, 0, 1).reshape(Bn, T, MLA_OUT)


def rwkv7_scan(r, w, k, v, a, b, reverse):
    Bn, T, H, N = r.shape

    def step(S, inp):
        r_t, w_t, k_t, v_t, a_t, b_t = inp
        sa = jnp.einsum('bhij,bhj->bhi', S, a_t)
        S = S * w_t[:, :, None, :] + sa[..., None] * b_t[:, :, None, :] + v_t[..., None] * k_t[:, :, None, :]
        return S, jnp.einsum('bhij,bhj->bhi', S, r_t)

    xs = tuple(jnp.moveaxis(z, 1, 0) for z in (r, w, k, v, a, b))
    _, y = lax.scan(step, jnp.zeros((Bn, H, N, N), jnp.float32), xs, reverse=reverse)
    return jnp.moveaxis(y, 0, 1)


def rwkv7_mixer(z, w0, w2, a0, a2, g2, k_k, k_a, r_k, gn_g, gn_b):
    Bn, T, _ = z.shape
    H, N = RWKV_HEADS, RWKV_HEAD_DIM
    f32 = jnp.float32
    r, k, v, wd, ad, gd = jnp.split(z, [RWKV_WIDTH, 2 * RWKV_WIDTH, 3 * RWKV_WIDTH,
                                        3 * RWKV_WIDTH + 2 * DECAY_LORA,
                                        3 * RWKV_WIDTH + 2 * DECAY_LORA + 2 * AAA_LORA], axis=-1)
    wd = wd.reshape(Bn, T, 2, DECAY_LORA)
    ad = ad.reshape(Bn, T, 2, AAA_LORA)
    log_w = -RWKV_DECAY_SCALE * jax.nn.sigmoid((w0 + jnp.einsum('btzl,zlc->btzc', jnp.tanh(wd), w2)).astype(f32))
    decay = jnp.exp(log_w).reshape(Bn, T, 2, H, N)
    a = jax.nn.sigmoid((a0 + jnp.einsum('btzl,zlc->btzc', ad, a2)).astype(f32)).reshape(Bn, T, 2, H, N)
    g = jax.nn.sigmoid(gd) @ g2
    rh = r.reshape(Bn, T, H, N).astype(f32)
    kh = k.reshape(Bn, T, H, N).astype(f32)
    vh = v.reshape(Bn, T, H, N).astype(f32)
    kk = l2_normalize(kh * k_k.reshape(H, N))
    k_dir = kh[:, :, None] * (1.0 + (a - 1.0) * k_a.reshape(H, N))
    b_dir = kk[:, :, None] * a
    y = (rwkv7_scan(rh, decay[:, :, 0], k_dir[:, :, 0], vh, -kk, b_dir[:, :, 0], False)
         + rwkv7_scan(rh, decay[:, :, 1], k_dir[:, :, 1], vh, -kk, b_dir[:, :, 1], True))
    mu = jnp.mean(y, -1, keepdims=True)
    var = jnp.mean(jnp.square(y - mu), -1, keepdims=True)
    y = (y - mu) * lax.rsqrt(var + RWKV_GN_EPS) * gn_g.reshape(H, N) + gn_b.reshape(H, N)
    bonus = jnp.sum(rh[:, :, None] * k_dir * r_k, axis=(2, 4))[..., None] * vh
    return ((y + bonus).reshape(Bn, T, RWKV_WIDTH) * g).astype(z.dtype)


def even_mixer(h, positions, w_in, shift_mu, q_norm, w_uq, kv_norm, w_ukv, w0, w2, a0, a2, g2,
               k_k, k_a, r_k, gn_g, gn_b, w_out):
    p = h @ w_in
    p_mla, p_rwkv = p[..., :MLA_IN], p[..., MLA_IN:]
    p_rwkv = p_rwkv + (centred_shift(p_rwkv) - p_rwkv) * shift_mu
    y = jnp.concatenate([mla_mixer(p_mla, positions, q_norm, w_uq, kv_norm, w_ukv),
                         rwkv7_mixer(p_rwkv, w0, w2, a0, a2, g2, k_k, k_a, r_k, gn_g, gn_b)], axis=-1)
    return y @ w_out


def depthwise_conv(x, w):
    return lax.conv_general_dilated(x, w[:, None, :], window_strides=(1,), padding='SAME',
                                    dimension_numbers=('NWC', 'WIO', 'NWC'),
                                    feature_group_count=x.shape[-1])


def gated_delta_chunked(q, k, v, g, beta):
    in_dtype = v.dtype
    f32 = jnp.float32
    q, k, v, g, beta = (z.astype(f32) for z in (q, k, v, g, beta))
    Bn, H, T, DK = q.shape
    DV = v.shape[-1]
    n = T // CHUNK
    q = q.reshape(Bn, H, n, CHUNK, DK)
    k = k.reshape(Bn, H, n, CHUNK, DK)
    v = v.reshape(Bn, H, n, CHUNK, DV)
    g = jnp.cumsum(g.reshape(Bn, H, n, CHUNK), axis=-1)
    beta = beta.reshape(Bn, H, n, CHUNK)[..., None]
    k_beta = k * beta
    incl = jnp.tril(jnp.ones((CHUNK, CHUNK), bool))
    strict = jnp.tril(jnp.ones((CHUNK, CHUNK), bool), -1)
    decay = jnp.where(incl, jnp.exp(jnp.where(incl, g[..., :, None] - g[..., None, :], 0.0)), 0.0)
    eye = jnp.eye(CHUNK, dtype=f32)
    lower = jnp.where(strict, jnp.einsum('bhncd,bhnmd->bhncm', k_beta, k) * decay, 0.0)
    t_inv = lax.linalg.triangular_solve(eye + lower, jnp.broadcast_to(eye, lower.shape),
                                        left_side=True, lower=True, unit_diagonal=True)
    u = t_inv @ (v * beta)
    w = t_inv @ (k_beta * jnp.exp(g)[..., None])
    attn = jnp.where(incl, jnp.einsum('bhncd,bhnmd->bhncm', q, k) * decay, 0.0)
    q_g = q * jnp.exp(g)[..., None]
    k_tail = k * jnp.exp(g[..., -1:] - g)[..., None]
    g_last = jnp.exp(g[..., -1])[..., None, None]

    def step(S, inp):
        u_i, w_i, a_i, qg_i, kt_i, gl_i = inp
        v_new = u_i - w_i @ S
        o_i = qg_i @ S + a_i @ v_new
        S = S * gl_i + jnp.swapaxes(kt_i, -1, -2) @ v_new
        return S, o_i

    xs = tuple(jnp.moveaxis(z, 2, 0) for z in (u, w, attn, q_g, k_tail, g_last))
    _, o = lax.scan(step, jnp.zeros((Bn, H, DK, DV), f32), xs)
    return jnp.moveaxis(o, 0, 2).reshape(Bn, H, T, DV).astype(in_dtype)


def odd_mixer(h, w_in, conv_w, a_log, dt_bias, norm_g, w_out):
    Bn, T, _ = h.shape
    f32 = jnp.float32
    p = h @ w_in
    qkv, z, b, a = jnp.split(p, [GDN_QKV, GDN_QKV + GDN_VAL_WIDTH,
                                 GDN_QKV + GDN_VAL_WIDTH + 2 * GDN_V_HEADS], axis=-1)
    qkv = jax.nn.silu(depthwise_conv(qkv, conv_w))
    q, k, v = jnp.split(qkv, [GDN_KEY_WIDTH, 2 * GDN_KEY_WIDTH], axis=-1)
    rep = GDN_V_HEADS // GDN_K_HEADS
    q = jnp.repeat(l2_normalize(q.reshape(Bn, T, GDN_K_HEADS, GDN_DK)), rep, axis=2) * GDN_DK ** -0.5
    k = jnp.repeat(l2_normalize(k.reshape(Bn, T, GDN_K_HEADS, GDN_DK)), rep, axis=2)
    v = v.reshape(Bn, T, GDN_V_HEADS, GDN_DV)
    beta = jax.nn.sigmoid(b.astype(f32)).reshape(Bn, T, 2, GDN_V_HEADS)
    g = -jnp.exp(a_log) * jax.nn.softplus(a.astype(f32).reshape(Bn, T, 2, GDN_V_HEADS) + dt_bias)
    qh, kh, vh = (jnp.swapaxes(t, 1, 2) for t in (q, k, v))
    gh = jnp.moveaxis(g, 1, 3)
    bh = jnp.moveaxis(beta, 1, 3)

    def flip(t):
        return jnp.flip(t, axis=2)

    o_fwd = gated_delta_chunked(qh, kh, vh, gh[:, 0], bh[:, 0])
    o_bwd = flip(gated_delta_chunked(flip(qh), flip(kh), flip(vh), flip(gh[:, 1]), flip(bh[:, 1])))
    o = jnp.swapaxes(o_fwd + o_bwd, 1, 2)
    o = rms_norm(o, norm_g) * jax.nn.silu(z.reshape(Bn, T, GDN_V_HEADS, GDN_DV))
    return o.reshape(Bn, T, GDN_VAL_WIDTH) @ w_out


def expert_choice_ffn(h, w_router, w_gate, w_up, w_down):
    Bn, T, D = h.shape
    cap = CAPACITY_FACTOR * T // N_EXPERTS
    aff = jax.nn.softmax((h @ w_router).astype(jnp.float32), axis=-1)
    gate, idx = lax.top_k(jnp.swapaxes(aff, 1, 2), cap)
    xs = jax.vmap(lambda hb, ib: hb[ib])(h, idx)
    hid = jax.nn.silu(jnp.einsum('becd,edf->becf', xs, w_gate)) * jnp.einsum('becd,edf->becf', xs, w_up)
    ys = jnp.einsum('becf,efd->becd', hid, w_down) * gate[..., None].astype(h.dtype)
    return jax.vmap(lambda ib, yb: jnp.zeros((T, D), h.dtype).at[ib.reshape(-1)].add(yb.reshape(-1, D)))(idx, ys)


def setup_inputs(seed: int = 0) -> dict:
    key = jax.random.key(seed)
    ks = iter(jax.random.split(key, 48))

    def nrm(shape, scale):
        return scale * jax.random.normal(next(ks), shape, jnp.float32)

    def unif(shape, lo, hi):
        return jax.random.uniform(next(ks), shape, jnp.float32, lo, hi)

    D = D_MODEL
    x = nrm((BATCH, SEQ, D), 1.0)
    c = nrm((BATCH, D), 1.0)
    positions = jax.random.randint(next(ks), (BATCH, 1), 0, SEQ, jnp.int32) + jnp.arange(SEQ, dtype=jnp.int32)
    ada_w = nrm((DEPTH, D, 6 * D), 0.5 * D ** -0.5)
    ada_b = nrm((DEPTH, 6 * D), 0.01)
    ln_g = 1.0 + nrm((DEPTH, 2, D), 0.02)
    ln_b = nrm((DEPTH, 2, D), 0.02)
    e_w_in = nrm((N_EVEN, D, EVEN_IN), D ** -0.5)
    e_shift_mu = unif((N_EVEN, RWKV_IN), 0.0, 1.0)
    mla_q_norm = 1.0 + nrm((N_EVEN, Q_LORA), 0.02)
    mla_w_uq = nrm((N_EVEN, Q_LORA, MLA_HEADS * (NOPE_DIM + ROPE_DIM)), Q_LORA ** -0.5)
    mla_kv_norm = 1.0 + nrm((N_EVEN, KV_LORA), 0.02)
    mla_w_ukv = nrm((N_EVEN, KV_LORA, MLA_HEADS * (NOPE_DIM + V_DIM)), KV_LORA ** -0.5)
    rwkv_w0 = nrm((N_EVEN, 2, RWKV_WIDTH), 1.0)
    rwkv_w2 = nrm((N_EVEN, 2, DECAY_LORA, RWKV_WIDTH), 0.1 * DECAY_LORA ** -0.5)
    rwkv_a0 = nrm((N_EVEN, 2, RWKV_WIDTH), 0.5)
    rwkv_a2 = nrm((N_EVEN, 2, AAA_LORA, RWKV_WIDTH), 0.1 * AAA_LORA ** -0.5)
    rwkv_g2 = nrm((N_EVEN, GATE_LORA, RWKV_WIDTH), GATE_LORA ** -0.5)
    rwkv_k_k = 0.85 + nrm((N_EVEN, RWKV_WIDTH), 0.02)
    rwkv_k_a = 1.0 + nrm((N_EVEN, RWKV_WIDTH), 0.02)
    rwkv_r_k = nrm((N_EVEN, RWKV_HEADS, RWKV_HEAD_DIM), 0.1)
    rwkv_gn_g = 1.0 + nrm((N_EVEN, RWKV_WIDTH), 0.02)
    rwkv_gn_b = nrm((N_EVEN, RWKV_WIDTH), 0.02)
    e_w_out = nrm((N_EVEN, EVEN_MIX, D), DEEPNORM_BETA * EVEN_MIX ** -0.5)
    o_w_in = nrm((N_ODD, D, ODD_IN), D ** -0.5)
    gdn_conv = nrm((N_ODD, CONV_WIDTH, GDN_QKV), CONV_WIDTH ** -0.5)
    gdn_a_log = jnp.log(unif((N_ODD, 2, GDN_V_HEADS), 1.0, 16.0))
    dt = jnp.exp(unif((N_ODD, 2, GDN_V_HEADS), math.log(1e-3), math.log(0.1)))
    gdn_dt_bias = dt + jnp.log(-jnp.expm1(-dt))
    gdn_norm = 1.0 + nrm((N_ODD, GDN_DV), 0.02)
    o_w_out = nrm((N_ODD, GDN_VAL_WIDTH, D), DEEPNORM_BETA * GDN_VAL_WIDTH ** -0.5)
    moe_router = nrm((DEPTH, D, N_EXPERTS), D ** -0.5)
    moe_w_gate = nrm((DEPTH, N_EXPERTS, D, EXPERT_FF), D ** -0.5)
    moe_w_up = nrm((DEPTH, N_EXPERTS, D, EXPERT_FF), D ** -0.5)
    moe_w_down = nrm((DEPTH, N_EXPERTS, EXPERT_FF, D), DEEPNORM_BETA * EXPERT_FF ** -0.5)
    return {'x': x, 'c': c, 'positions': positions, 'ada_w': ada_w, 'ada_b': ada_b,
            'ln_g': ln_g, 'ln_b': ln_b, 'e_w_in': e_w_in, 'e_shift_mu': e_shift_mu,
            'mla_q_norm': mla_q_norm, 'mla_w_uq': mla_w_uq, 'mla_kv_norm': mla_kv_norm,
            'mla_w_ukv': mla_w_ukv, 'rwkv_w0': rwkv_w0, 'rwkv_w2': rwkv_w2, 'rwkv_a0': rwkv_a0,
            'rwkv_a2': rwkv_a2, 'rwkv_g2': rwkv_g2, 'rwkv_k_k': rwkv_k_k, 'rwkv_k_a': rwkv_k_a,
            'rwkv_r_k': rwkv_r_k, 'rwkv_gn_g': rwkv_gn_g, 'rwkv_gn_b': rwkv_gn_b, 'e_w_out': e_w_out,
            'o_w_in': o_w_in, 'gdn_conv': gdn_conv, 'gdn_a_log': gdn_a_log, 'gdn_dt_bias': gdn_dt_bias,
            'gdn_norm': gdn_norm, 'o_w_out': o_w_out, 'moe_router': moe_router,
            'moe_w_gate': moe_w_gate, 'moe_w_up': moe_w_up, 'moe_w_down': moe_w_down}


def reference(x, c, positions, ada_w, ada_b, ln_g, ln_b, e_w_in, e_shift_mu, mla_q_norm, mla_w_uq,
              mla_kv_norm, mla_w_ukv, rwkv_w0, rwkv_w2, rwkv_a0, rwkv_a2, rwkv_g2, rwkv_k_k, rwkv_k_a,
              rwkv_r_k, rwkv_gn_g, rwkv_gn_b, e_w_out, o_w_in, gdn_conv, gdn_a_log, gdn_dt_bias,
              gdn_norm, o_w_out, moe_router, moe_w_gate, moe_w_up, moe_w_down):
    cond = jax.nn.silu(c)
    for i in range(DEPTH):
        mod = (cond @ ada_w[i] + ada_b[i])[:, None, :]
        sh_m, sc_m, g_m, sh_f, sc_f, g_f = jnp.split(mod, 6, axis=-1)
        h = x * (1.0 + sc_m) + sh_m
        j = i // 2
        if i % 2 == 0:
            y = even_mixer(h, positions, e_w_in[j], e_shift_mu[j], mla_q_norm[j], mla_w_uq[j],
                           mla_kv_norm[j], mla_w_ukv[j], rwkv_w0[j], rwkv_w2[j], rwkv_a0[j], rwkv_a2[j],
                           rwkv_g2[j], rwkv_k_k[j], rwkv_k_a[j], rwkv_r_k[j], rwkv_gn_g[j], rwkv_gn_b[j],
                           e_w_out[j])
        else:
            y = odd_mixer(h, o_w_in[j], gdn_conv[j], gdn_a_log[j], gdn_dt_bias[j], gdn_norm[j], o_w_out[j])
        x = layer_norm(DEEPNORM_ALPHA * x + g_m * y, ln_g[i, 0], ln_b[i, 0])
        h = x * (1.0 + sc_f) + sh_f
        y = expert_choice_ffn(h, moe_router[i], moe_w_gate[i], moe_w_up[i], moe_w_down[i])
        x = layer_norm(DEEPNORM_ALPHA * x + g_f * y, ln_g[i, 1], ln_b[i, 1])
    return x
```

```python
import time
import numpy as np
import ml_dtypes
from concourse.bass_utils import run_bass_kernel_spmd
import numpy as np
import concourse.bass as bass
import concourse.mybir as mybir
from contextlib import ExitStack

F32 = mybir.dt.float32
BF16 = mybir.dt.bfloat16
I32 = mybir.dt.int32
AF = mybir.ActivationFunctionType
ALU = mybir.AluOpType
AX = mybir.AxisListType

ENG = ["pe", "act", "dve", "pool", "sp"]
NDS = 40


class T:
    __slots__ = ("ap", "w", "r", "name")

    def __init__(self, ap, name=""):
        self.ap = ap
        self.w = None
        self.r = {}
        self.name = name

    def __getitem__(self, idx):
        return self.ap[idx]


class Prog:
    def __init__(self, nc, es):
        self.nc = nc
        self.es = es
        self.ops = {e: [] for e in ENG}
        self.sem = {e: es.enter_context(nc.semaphore("s_" + e)) for e in ENG}
        self.cnt = {e: 0 for e in ENG}
        self.dsem = [es.enter_context(nc.semaphore("d%d" % i)) for i in range(NDS)]
        self.dval = [0] * NDS
        self.dnext = 0
        self.waited = {e: {} for e in ENG}
        self.semobj = {}
        self.n_alloc = 0

    def sb(self, shape, dt, name=None):
        self.n_alloc += 1
        name = name or ("t%d" % self.n_alloc)
        h = self.es.enter_context(self.nc.sbuf_tensor(name, list(shape), dt))
        return T(h[tuple(slice(None) for _ in shape)], name)

    def ps(self, shape, dt=F32, name=None):
        self.n_alloc += 1
        name = name or ("p%d" % self.n_alloc)
        h = self.es.enter_context(self.nc.psum_tensor(name, list(shape), dt))
        return T(h[tuple(slice(None) for _ in shape)], name)

    def view(self, ap, name=""):
        return T(ap, name)

    def _deps(self, eng, reads, writes):
        toks = {}

        def add(tok):
            if tok is None:
                return
            s, v = tok
            if toks.get(s, 0) < v:
                toks[s] = v

        for t in reads:
            add(t.w)
        for t in writes:
            add(t.w)
            for s, v in t.r.items():
                add((s, v))
        waits = []
        wd = self.waited[eng]
        for s, v in toks.items():
            if eng == "pe" and s == "s_pe":
                continue
            if wd.get(s, 0) >= v:
                continue
            wd[s] = v
            waits.append((s, v))
        return waits

    def _sem(self, sname):
        if sname.startswith("s_"):
            return self.sem[sname[2:]]
        return self.dsem[int(sname[1:])]

    def op(self, eng, emit, reads=(), writes=()):
        waits = self._deps(eng, reads, writes)
        self.cnt[eng] += 1
        tok = ("s_" + eng, self.cnt[eng])
        for t in reads:
            if t.r.get(tok[0], 0) < tok[1]:
                t.r[tok[0]] = tok[1]
        for t in writes:
            t.w = tok
            t.r = {}
        self.ops[eng].append((waits, emit, tok[0], 1))
        return tok

    def dma(self, eng, emit, reads=(), writes=()):
        slot = self.dnext
        self.dnext = (self.dnext + 1) % NDS
        sname = "d%d" % slot
        waits = self._deps(eng, reads, writes)
        prev = self.dval[slot]
        wd = self.waited[eng]
        if prev > 0 and wd.get(sname, 0) < prev:
            wd[sname] = prev
            waits.append((sname, prev))
        self.dval[slot] = prev + 16
        tok = (sname, prev + 16)
        for t in reads:
            if t.r.get(tok[0], 0) < tok[1]:
                t.r[tok[0]] = tok[1]
        for t in writes:
            t.w = tok
            t.r = {}
        self.ops[eng].append((waits, emit, sname, 16))
        return tok

    def wait_all(self, eng, tiles):
        waits = self._deps(eng, tiles, ())
        self.ops[eng].append((waits, None, None, 0))

    def emit(self):
        nc = self.nc
        prog = self

        def run(engname, e):
            for waits, emit, sname, inc in prog.ops[engname]:
                for s, v in waits:
                    e.wait_ge(prog._sem(s), v)
                if emit is not None:
                    ins = emit(e)
                    ins.then_inc(prog._sem(sname), inc)

        with nc.Block() as block:
            @block.tensor
            def _(e):
                run("pe", e)

            @block.scalar
            def _(e):
                run("act", e)

            @block.vector
            def _(e):
                run("dve", e)

            @block.gpsimd
            def _(e):
                run("pool", e)

            @block.sync
            def _(e):
                run("sp", e)

    def mm(self, out_t, out_ap, lhsT_t, lhsT_ap, rhs_t, rhs_ap, start, stop, extra_reads=()):
        return self.op("pe", lambda e: e.matmul(out_ap, lhsT_ap, rhs_ap, start=start, stop=stop),
                       reads=[lhsT_t, rhs_t, *extra_reads], writes=[out_t])

    def load(self, dst_t, dst_ap, src_t, src_ap, eng="sp", **kw):
        return self.dma(eng, lambda e: e.dma_start(out=dst_ap, in_=src_ap, **kw), reads=[src_t], writes=[dst_t])

    def do(self, eng, method, *args, reads=(), writes=(), **kw):
        return self.op(eng, lambda e: getattr(e, method)(*args, **kw), reads=reads, writes=writes)

    def barrier(self):
        for eng in ENG:
            waits = []
            wd = self.waited[eng]
            for e2 in ENG:
                if e2 == "pe" and eng == "pe":
                    continue
                v = self.cnt[e2]
                s = "s_" + e2
                if v > 0 and wd.get(s, 0) < v:
                    wd[s] = v
                    waits.append((s, v))
            for i, v in enumerate(self.dval):
                s = "d%d" % i
                if v > 0 and wd.get(s, 0) < v:
                    wd[s] = v
                    waits.append((s, v))
            self.ops[eng].append((waits, None, None, 0))

    def _split(self, args):
        tiles, aps = [], []
        for a in args:
            if isinstance(a, tuple):
                tiles.append(a[0]); aps.append(a[1])
            elif isinstance(a, T):
                tiles.append(a); aps.append(a.ap)
            else:
                aps.append(a)
        return tiles, aps

    def V(self, eng, method, out, *ins, **kw):
        ot, oa = self._split([out])
        it, ia = self._split(ins)
        kt = []
        kw2 = {}
        for k, v in kw.items():
            if isinstance(v, tuple) and isinstance(v[0], T):
                kt.append(v[0]); kw2[k] = v[1]
            elif isinstance(v, T):
                kt.append(v); kw2[k] = v.ap
            else:
                kw2[k] = v
        return self.op(eng, lambda e: getattr(e, method)(oa[0], *ia, **kw2), reads=it + kt, writes=ot)

import numpy as np

def linear_fm(P, xT_t, xT, K, Tn, W_t, W, N, outs, TB=2048, func=None, evac=None):
    KC = (K + 127) // 128
    kp = min(K, 128)
    TB = min(TB, Tn)
    with ExitStack() as es2:
        old = P.es; P.es = es2
        xs = P.sb([128, KC, TB], BF16)
        wf = [P.sb([128, KC, 128], F32) for _ in range(2)]
        wb = [P.sb([128, KC, 128], BF16) for _ in range(2)]
        ob = {}
        pss = [P.ps([128, 512], F32) for _ in range(4)]
        pi = 0; wi = 0
        for t0 in range(0, Tn, TB):
            for kc in range(KC):
                P.load(xs, xs[:kp, kc, :], xT_t, xT[kc * 128:kc * 128 + kp, t0:t0 + TB], eng=("sp" if kc % 2 == 0 else "pool"))
            for (n0, n1, out_t, out_ap, odt, kw) in outs:
                for c0 in range(n0, n1, 128):
                    cw = min(128, n1 - c0)
                    f, b = wf[wi % 2], wb[wi % 2]; wi += 1
                    for kc in range(KC):
                        P.load(f, f[:kp, kc, :cw], W_t, W[kc * 128:kc * 128 + kp, c0:c0 + cw], eng="sp")
                    P.do("pool", "tensor_copy", b[:kp, :, :cw], f[:kp, :, :cw], reads=[f], writes=[b])
                    key = (odt, )
                    if key not in ob:
                        ob[key] = [P.sb([128, TB], odt) for _ in range(2)]
                    o = ob[key][wi % 2]
                    for s0 in range(0, TB, 512):
                        sw = min(512, TB - s0)
                        ps = pss[pi % 4]; pi += 1
                        for kc in range(KC):
                            P.mm(ps, ps[:cw, :sw], b, b[:kp, kc, :cw], xs, xs[:kp, kc, s0:s0 + sw], kc == 0, kc == KC - 1)
                        fn = kw.get('func', AF.Identity)
                        rd = [ps]
                        akw = {}
                        for nm in ('bias', 'scale'):
                            if nm in kw:
                                v = kw[nm]
                                if isinstance(v, tuple):
                                    rd.append(v[0]); akw[nm] = v[1][c0 - n0:c0 - n0 + cw, :]
                                else:
                                    akw[nm] = v
                        P.do("act", "activation", o[:cw, s0:s0 + sw], ps[:cw, :sw], fn, reads=rd, writes=[o], **akw)
                    P.load(out_t, out_ap[c0 - n0:c0 - n0 + cw, t0:t0 + TB], o, o[:cw, :TB], eng="pool")
        P.barrier()
        P.es = old


import numpy as np
C = 64

def scan_core(P, npairs, nch, dk, dv, WT, RT, ARB, TAT, ARK, BP, KP, V, PC, Y, CH=None, has_k=True, revs=None):
    nc = P.nc
    revs = revs or [False] * npairs
    CH = CH or min(nch, 32)
    ngrp = nch // CH
    nfm_per = 1 if dk == 64 else 2
    NB = 2
    with ExitStack() as es2:
        old = P.es; P.es = es2
        bufs = []
        for i in range(NB):
            d = {}
            d['wt'] = [P.sb([128, CH * 64], BF16) for _ in range(nfm_per)]
            d['rt'] = [P.sb([128, CH * 64], BF16) for _ in range(nfm_per)]
            d['arb'] = P.sb([128, CH, 64], BF16)
            d['tat'] = P.sb([128, CH, 64], BF16)
            d['ark'] = P.sb([128, CH, 64], BF16)
            d['bp'] = P.sb([128, CH, dk], BF16)
            d['kp'] = P.sb([128, CH, dk], BF16)
            d['v'] = P.sb([128, CH, dv], BF16)
            d['pc'] = [P.sb([128, CH], F32) for _ in range(nfm_per)]
            d['y'] = P.sb([128, CH, dv], F32)
            d['z'] = [P.sb([128, dv], F32) for _ in range(nfm_per)]
            d['zb'] = [P.sb([128, dv], BF16) for _ in range(nfm_per)]
            d['u'] = P.sb([128, dv], BF16)
            d['psu'] = P.ps([128, dv], F32)
            d['psy'] = P.ps([128, dv], F32)
            d['psz'] = [P.ps([128, dv], F32) for _ in range(nfm_per)]
            bufs.append(d)
        dts = {k: (v if isinstance(v, T) else T(v)) for k, v in dict(WT=WT, RT=RT, ARB=ARB, TAT=TAT, ARK=ARK, BP=BP, KP=KP, V=V, PC=PC).items() if v is not None}
        WT, RT, ARB, TAT, BP, V, PC = [x.ap if isinstance(x, T) else x for x in (WT, RT, ARB, TAT, BP, V, PC)]
        if has_k:
            ARK, KP = [x.ap if isinstance(x, T) else x for x in (ARK, KP)]
        Yt = Y if isinstance(Y, T) else T(Y)
        Y = Yt.ap
        for p0 in range(0, npairs, NB):
            prs = list(range(p0, min(p0 + NB, npairs)))
            for pi, pr in enumerate(prs):
                d = bufs[pi]
                for z in d['z']:
                    P.do("dve", "memset", z[:, :], 0.0, writes=[z])
                for zb in d['zb']:
                    P.do("pool", "memset", zb[:, :], 0.0, writes=[zb])
            for g in range(ngrp):
                for pi, pr in enumerate(prs):
                    d = bufs[pi]
                    gg = (ngrp - 1 - g) if revs[pr] else g
                    cs = slice(gg * CH, (gg + 1) * CH)
                    q = "sp" if pi % 2 == 0 else "pool"
                    for j in range(nfm_per):
                        f = pr * nfm_per + j
                        P.load(d['wt'][j], d['wt'][j][:, :], dts['WT'], WT[f, :, gg * CH * 64:(gg + 1) * CH * 64], eng=q)
                        P.load(d['rt'][j], d['rt'][j][:, :], dts['RT'], RT[f, :, gg * CH * 64:(gg + 1) * CH * 64], eng=q)
                        P.load(d['pc'][j], d['pc'][j][:, :], dts['PC'], PC[f, :, cs], eng=q)
                    for nm, src in ((('arb', ARB), ('tat', TAT), ('ark', ARK), ('bp', BP), ('kp', KP), ('v', V)) if has_k else (('arb', ARB), ('tat', TAT), ('bp', BP), ('v', V))):
                        P.load(d[nm], d[nm][:, :, :], dts[nm.upper()], src[pr, :, cs, :], eng=q)
                for ci in range(CH):
                    for pi, pr in enumerate(prs):
                        d = bufs[pi]
                        c = (CH - 1 - ci) if revs[pr] else ci
                        for h in range(2):
                            hs = slice(h * 64, (h + 1) * 64)
                            if dk == 64:
                                wt, zb = d['wt'][0], d['zb'][0]
                                P.mm(d['psu'], d['psu'][hs, :], wt, wt[hs, c * 64:(c + 1) * 64], zb, zb[hs, :], True, False)
                            else:
                                wt, zb = d['wt'][h], d['zb'][h]
                                P.mm(d['psu'], d['psu'][hs, :], wt, wt[:, c * 64:(c + 1) * 64], zb, zb[:, :], True, False)
                            P.mm(d['psu'], d['psu'][hs, :], d['tat'], d['tat'][hs, c, :], d['v'], d['v'][hs, c, :], False, True)
                        P.do("act", "copy", d['u'][:, :], d['psu'][:, :], reads=[d['psu']], writes=[d['u']])
                        for h in range(2):
                            hs = slice(h * 64, (h + 1) * 64)
                            if dk == 64:
                                rt, zb = d['rt'][0], d['zb'][0]
                                P.mm(d['psy'], d['psy'][hs, :], rt, rt[hs, c * 64:(c + 1) * 64], zb, zb[hs, :], True, False)
                            else:
                                rt, zb = d['rt'][h], d['zb'][h]
                                P.mm(d['psy'], d['psy'][hs, :], rt, rt[:, c * 64:(c + 1) * 64], zb, zb[:, :], True, False)
                            if has_k:
                                P.mm(d['psy'], d['psy'][hs, :], d['ark'], d['ark'][hs, c, :], d['v'], d['v'][hs, c, :], False, False)
                            P.mm(d['psy'], d['psy'][hs, :], d['arb'], d['arb'][hs, c, :], d['u'], d['u'][hs, :], False, True)
                        P.do("act", "copy", d['y'][:, c, :], d['psy'][:, :], reads=[d['psy']], writes=[d['y']])
                        for h in range(2):
                            hs = slice(h * 64, (h + 1) * 64)
                            if dk == 64:
                                pz, po = d['psz'][0], d['psz'][0][hs, :]
                            else:
                                pz, po = d['psz'][h], d['psz'][h][:, :]
                            if has_k:
                                P.mm(pz, po, d['kp'], d['kp'][hs, c, :], d['v'], d['v'][hs, c, :], True, False)
                            P.mm(pz, po, d['bp'], d['bp'][hs, c, :], d['u'], d['u'][hs, :], not has_k, True)
                        for j in range(nfm_per):
                            z, zb, pz, pc = d['z'][j], d['zb'][j], d['psz'][j], d['pc'][j]
                            P.do("dve", "scalar_tensor_tensor", z[:, :], z[:, :], pc[:, c:c + 1], pz[:, :], ALU.mult, ALU.add,
                                 reads=[z, pc, pz], writes=[z])
                            P.do("act", "copy", zb[:, :], z[:, :], reads=[z], writes=[zb])
                for pi, pr in enumerate(prs):
                    d = bufs[pi]
                    gg = (ngrp - 1 - g) if revs[pr] else g
                    P.load(Yt, Y[pr, :, gg * CH:(gg + 1) * CH, :], d['y'], d['y'][:, :, :], eng="sp")
        P.barrier()
        P.es = old
    return Yt

import numpy as np
NB = 8

def dram(nc, name, shape, dt):
    return nc.dram_tensor(name, list(shape), dt).ap()

def bc3(ap2, n):
    return ap2.unsqueeze(2).to_broadcast([ap2.shape[0], ap2.shape[1], n])

def inv_bufs(P, nb):
    return dict(A2=P.sb([128, nb, 64], F32), B2=P.sb([128, nb, 64], F32), G=[P.sb([128, nb, 64], F32), P.sb([128, nb, 64], F32)],
                ps=[P.ps([128, nb, 64]), P.ps([128, nb, 64]), P.ps([128, nb, 64])])


def inverse_T(P, A, B, ident, nb, ib, lvl=5):
    Am = [A, ib['A2']]
    Bm = [B, ib['B2']]
    G = ib['G']
    psA, psB, psG = ib['ps']
    P.V("dve", "tensor_tensor", G[0], ident, B, ALU.add)
    cur = 0
    for l in range(lvl):
        a, b, a2, b2 = Am[cur], Bm[cur], Am[1 - cur], Bm[1 - cur]
        for i in range(nb):
            for h in range(2):
                hs = slice(h * 64, h * 64 + 64)
                P.mm(psA, psA[hs, i, :], b, b[hs, i, :], a, a[hs, i, :], True, True)
        P.V("act", "copy", a2, psA)
        if l < lvl - 1:
            for i in range(nb):
                for h in range(2):
                    hs = slice(h * 64, h * 64 + 64)
                    P.mm(psB, psB[hs, i, :], a, a[hs, i, :], b, b[hs, i, :], True, True)
            P.V("dve", "tensor_copy", b2, psB)
        g, g2 = G[l % 2], G[1 - l % 2]
        for i in range(nb):
            for h in range(2):
                hs = slice(h * 64, h * 64 + 64)
                P.mm(psG, psG[hs, i, :], a2, a2[hs, i, :], g, g[hs, i, :], True, True)
        P.V("dve", "tensor_tensor", g2, g, psG, ALU.add)
        cur = 1 - cur
    return G[lvl % 2]

def cumsum64(P, x, rows, nch, reverse, y=None):
    y = y or P.sb([rows, nch * 64], F32)
    bufs = [x, y]
    cur = 0
    for s in (1, 2, 4, 8, 16, 32):
        a, b = bufs[cur], bufs[1 - cur]
        a3 = a.ap.rearrange("p (c t) -> p c t", t=64)
        b3 = b.ap.rearrange("p (c t) -> p c t", t=64)
        if not reverse:
            P.V("dve", "tensor_tensor", (b, b3[:, :, s:]), (a, a3[:, :, s:]), (a, a3[:, :, :64 - s]), ALU.add)
            P.V("pool", "tensor_copy", (b, b3[:, :, :s]), (a, a3[:, :, :s]))
        else:
            P.V("dve", "tensor_tensor", (b, b3[:, :, :64 - s]), (a, a3[:, :, :64 - s]), (a, a3[:, :, s:]), ALU.add)
            P.V("pool", "tensor_copy", (b, b3[:, :, 64 - s:]), (a, a3[:, :, 64 - s:]))
        cur = 1 - cur
    return bufs[cur]


def gdn_mixer(P, D, Tn, nkh, hT, W, convw, alog, dtb, normg, Wout, consts, yT):
    nc = P.nc
    nvh = 2 * nkh
    nch = Tn // 64
    NQ = nkh * 128
    NV = nvh * 128
    c_q, c_k, c_v, c_z, c_b, c_a = 0, NQ, 2 * NQ, 2 * NQ + NV, 2 * NQ + 2 * NV, 2 * NQ + 2 * NV + 2 * nvh
    qkvT = dram(nc, "g_qkvT", [2 * NQ + NV, Tn], F32); qkvT_t = T(qkvT)
    zT = dram(nc, "g_zT", [NV, Tn], BF16); zT_t = T(zT)
    bT = dram(nc, "g_bT", [2 * nvh, Tn], F32); bT_t = T(bT)
    aT = dram(nc, "g_aT", [2 * nvh, Tn], F32); aT_t = T(aT)
    hT_t, W_t = T(hT), T(W)
    linear_fm(P, hT_t, hT, D, Tn, W_t, W, c_a + 2 * nvh, [
        (0, c_z, qkvT_t, qkvT, F32, {}),
        (c_z, c_b, zT_t, zT, BF16, dict(func=AF.Silu)),
        (c_b, c_a, bT_t, bT, F32, dict(func=AF.Sigmoid)),
        (c_a, c_a + 2 * nvh, aT_t, aT, F32, {})])
    qkT = dram(nc, "g_qkT", [2 * NQ, Tn], BF16); qkT_t = T(qkT)
    vTb = dram(nc, "g_vTb", [NV, Tn], BF16); vTb_t = T(vTb)
    cw_t = T(convw)
    with ExitStack() as es2:
        old = P.es; P.es = es2
        ones = P.sb([128, 128], F32)
        P.load(ones, ones.ap, T(consts['ones128']), consts['ones128'])
        nchunks = (2 * NQ + NV) // 128
        xp = [P.sb([128, Tn + 4], F32) for _ in range(2)]
        acc = [P.sb([128, Tn], F32) for _ in range(2)]
        sq = P.sb([128, Tn], F32)
        ob = [P.sb([128, Tn], BF16) for _ in range(2)]
        cw = [P.sb([128, 5], F32) for _ in range(2)]
        ps = [P.ps([128, 512]) for _ in range(2)]
        rn = P.sb([128, 512], F32)
        eps6 = P.sb([128, 1], F32)
        P.V("dve", "memset", eps6, 1e-6)
        for b_ in xp:
            P.V("pool", "memset", (b_, b_[:, 0:2]), 0.0)
            P.V("pool", "memset", (b_, b_[:, Tn + 2:Tn + 4]), 0.0)
        for ci in range(nchunks):
            x_, a_, o_, w_ = xp[ci % 2], acc[ci % 2], ob[ci % 2], cw[ci % 2]
            P.load(x_, x_[:, 2:Tn + 2], qkvT_t, qkvT[ci * 128:(ci + 1) * 128, :])
            P.load(w_, w_.ap, cw_t, convw[ci * 128:(ci + 1) * 128, :], eng="pool")
            P.V("act", "activation", a_, (x_, x_[:, 0:Tn]), AF.Identity, scale=(w_, w_[:, 0:1]))
            for j in range(1, 5):
                P.V("dve", "scalar_tensor_tensor", a_, (x_, x_[:, j:j + Tn]), (w_, w_[:, j:j + 1]), a_, ALU.mult, ALU.add)
            if ci < 2 * nkh:
                P.V("act", "activation", a_, a_, AF.Silu)
                P.V("act", "activation", sq, a_, AF.Square)
                sc = (128 ** -0.5) if ci < nkh else 1.0
                for s0 in range(0, Tn, 512):
                    sw = min(512, Tn - s0)
                    p_ = ps[(s0 // 512) % 2]
                    P.mm(p_, p_[:, :sw], ones, ones.ap, sq, sq[:, s0:s0 + sw], True, True)
                    P.V("act", "activation", (rn, rn[:, :sw]), (p_, p_[:, :sw]), AF.Sqrt, bias=(eps6, eps6.ap))
                    P.V("dve", "reciprocal", (rn, rn[:, :sw]), (rn, rn[:, :sw]))
                    P.V("dve", "scalar_tensor_tensor", (o_, o_[:, s0:s0 + sw]), (a_, a_[:, s0:s0 + sw]), sc, (rn, rn[:, :sw]), ALU.mult, ALU.mult)
                P.load(qkT_t, qkT[ci * 128:(ci + 1) * 128, :], o_, o_.ap, eng="pool")
            else:
                P.V("act", "activation", o_, a_, AF.Silu)
                P.load(vTb_t, vTb[(ci - 2 * nkh) * 128:(ci - 2 * nkh + 1) * 128, :], o_, o_.ap, eng="pool")
        P.barrier(); P.es = old
    NSC = 5
    scT = dram(nc, "g_scT", [2, NSC, nvh, Tn], F32); scT_t = T(scT)
    pcT = dram(nc, "g_pcT", [2, nvh, nch], F32); pcT_t = T(pcT)
    al_t, dtb_t = T(alog), T(dtb)
    with ExitStack() as es2:
        old = P.es; P.es = es2
        araw = P.sb([nvh, Tn], F32); be = P.sb([nvh, Tn], F32)
        colA = P.sb([nvh, 1], F32); colD = P.sb([nvh, 1], F32)
        g = P.sb([nvh, Tn], F32); cy = P.sb([nvh, Tn], F32)
        egc = P.sb([nvh, Tn], F32); ebg = P.sb([nvh, Tn], F32); ktl = P.sb([nvh, Tn], F32); pc = P.sb([nvh, nch], F32)
        for d in range(2):
            P.load(araw, araw.ap, aT_t, aT[d * nvh:(d + 1) * nvh, :])
            P.load(be, be.ap, bT_t, bT[d * nvh:(d + 1) * nvh, :])
            P.load(colA, colA.ap, al_t, alog[d * nvh:(d + 1) * nvh, :])
            P.load(colD, colD.ap, dtb_t, dtb[d * nvh:(d + 1) * nvh, :])
            P.V("act", "activation", colA, colA, AF.Exp)
            P.V("dve", "tensor_scalar_mul", colA, colA, -1.0)
            P.V("act", "activation", araw, araw, AF.Exp, bias=(colD, colD.ap))
            P.V("dve", "tensor_scalar_add", araw, araw, 1.0)
            P.V("act", "activation", araw, araw, AF.Ln)
            P.V("dve", "tensor_scalar", g, araw, (colA, colA.ap), None, ALU.mult)
            gc = cumsum64(P, g, nvh, nch, reverse=(d == 1), y=cy)
            gc3 = gc.ap.rearrange("p (c t) -> p c t", t=64)
            last = gc3[:, :, 63:64] if d == 0 else gc3[:, :, 0:1]
            P.V("act", "activation", egc, gc, AF.Exp)
            P.V("dve", "tensor_tensor", ebg, egc, be, ALU.mult)
            k3 = ktl.ap.rearrange("p (c t) -> p c t", t=64)
            P.V("dve", "tensor_tensor", (ktl, k3), (gc, last.to_broadcast([nvh, nch, 64])), (gc, gc3), ALU.subtract)
            P.V("act", "activation", ktl, ktl, AF.Exp)
            P.V("act", "activation", (pc, pc.ap.unsqueeze(2)), (gc, last), AF.Exp)
            for k, tt in enumerate((gc, be, ebg, ktl, egc)):
                P.load(scT_t, scT[d, k, :, :], tt, tt.ap, eng=("sp" if k % 2 else "pool"))
            P.load(pcT_t, pcT[d, :, :], pc, pc.ap)
        P.barrier(); P.es = old
    return qkT, vTb, zT, scT, pcT, (qkT_t, vTb_t, zT_t, scT_t, pcT_t)


def gdn_chunks(P, Tn, nkh, qkT, vTb, scT, pcT, toks, consts, S):
    nc = P.nc
    nvh = 2 * nkh; nch = Tn // 64; NQ = nkh * 128
    qkT_t, vTb_t, zT_t, scT_t, pcT_t = toks
    St = {k: T(v) for k, v in S.items()}
    with ExitStack() as es2:
        old = P.es; P.es = es2
        cst = {}
        for k in ('ident2', 'mA0', 'mB0', 'mC0', 'mA1', 'mB1', 'mC1'):
            cst[k] = P.sb([128, NB, 64], F32)
            P.load(cst[k], cst[k].ap.rearrange("p a b -> p (a b)"), T(consts[k]), consts[k])
        id128 = P.sb([128, 128], BF16)
        P.load(id128, id128.ap, T(consts['ident128']), consts['ident128'])
        kT = P.sb([128, Tn], BF16); qT = P.sb([128, Tn], BF16); kT2 = P.sb([128, nch, 2, 64], BF16)
        vT = [P.sb([128, Tn], BF16) for _ in range(2)]
        ktok = P.sb([128, nch, 128], BF16)
        vtok = P.sb([128, nch, 128], BF16)
        psT = [P.ps([128, 4, 128])] * 2
        ib = inv_bufs(P, NB)
        psKK, psQK = P.ps([128, NB, 64]), P.ps([128, NB, 64])
        psW = [P.ps([128, NB, 64]) for _ in range(2)]
        cols = {k: P.sb([128, nch], F32) for k in ('gc', 'be', 'ebg', 'ktl')}
        gcbc = P.sb([128, nch, 64], F32); bebc = P.sb([128, nch, 64], F32)
        egcbc = [P.sb([128, Tn], F32)] * 2
        pcb = [P.sb([128, nch], F32) for _ in range(2)]
        E1 = P.sb([128, NB, 64], F32); E2 = P.sb([128, NB, 64], F32); tmp = P.sb([128, NB, 64], F32)
        A = P.sb([128, NB, 64], F32); B = P.sb([128, NB, 64], F32)
        arb = P.sb([128, NB, 64], BF16); tat = P.sb([128, NB, 64], BF16)
        atok = P.sb([128, NB, 128], F32); bp = P.sb([128, NB, 128], BF16)
        wt = [P.sb([128, NB, 64], BF16) for _ in range(2)]
        rt = [P.sb([128, Tn], BF16)] * 2
        for j in range(nkh):
            P.load(qT, qT.ap, qkT_t, qkT[j * 128:(j + 1) * 128, :])
            P.load(kT, kT.ap, qkT_t, qkT[NQ + j * 128:NQ + (j + 1) * 128, :], eng="pool")
            k3 = kT.ap.rearrange("p (c t) -> p c t", t=64)
            for h in range(2):
                P.V("pool", "tensor_copy", (kT2, kT2[:, :, h, :]), (kT, k3))
                P.load(vT[h], vT[h].ap, vTb_t, vTb[(2 * j + h) * 128:(2 * j + h + 1) * 128, :])
            for c0 in range(0, nch, 4):
                p_ = psT[(c0 // 4) % 2]
                n4 = min(4, nch - c0)
                for i in range(n4):
                    P.mm(p_, p_[:, i, :], kT2, kT2[:, c0 + i, :, :].rearrange("p a b -> p (a b)"), id128, id128.ap, True, True)
                P.V("act", "copy", (ktok, ktok[:, c0:c0 + n4, :]), (p_, p_[:, :n4, :]))
                p2 = psT[(c0 // 4 + 1) % 2]
                for i in range(n4):
                    for h in range(2):
                        hs = slice(h * 64, h * 64 + 64)
                        P.mm(p2, p2[hs, i, :], vT[h], vT[h][:, (c0 + i) * 64:(c0 + i + 1) * 64], id128, id128.ap, True, True)
                P.V("dve", "tensor_copy", (vtok, vtok[:, c0:c0 + n4, :]), (p2, p2[:, :n4, :]))
            for d in range(2):
                pd = j * 2 + d
                P.load(St['V'], S['V'][pd], vtok, vtok.ap, eng="pool")
                h0 = 2 * j
                for k, nm in enumerate(('gc', 'be', 'ebg', 'ktl')):
                    for h in range(2):
                        src = scT[d, k, h0 + h, :].rearrange("(c t) -> t c", t=64)
                        P.load(cols[nm], cols[nm][h * 64:(h + 1) * 64, :], scT_t, src, eng="sp", allow_slow_non_contiguous=True)
                for h in range(2):
                    hs = slice(h * 64, h * 64 + 64)
                    P.load(gcbc, gcbc[hs].rearrange("p c t -> p (c t)"), scT_t, scT[d, 0, h0 + h:h0 + h + 1, :].partition_broadcast(64) if False else scT[d, 0, h0 + h:h0 + h + 1, :].to_broadcast([64, Tn]), eng="pool")
                    P.load(bebc, bebc[hs].rearrange("p c t -> p (c t)"), scT_t, scT[d, 1, h0 + h:h0 + h + 1, :].to_broadcast([64, Tn]), eng="pool")
                    P.load(egcbc[h], egcbc[h].ap, scT_t, scT[d, 4, h0 + h:h0 + h + 1, :].to_broadcast([128, Tn]), eng="sp")
                    P.load(pcb[h], pcb[h].ap, pcT_t, pcT[d, h0 + h:h0 + h + 1, :].to_broadcast([128, nch]), eng="sp")
                    P.V("dve", "tensor_tensor", rt[h], qT, egcbc[h], ALU.mult)
                    P.load(St['RT'], S['RT'][pd * 2 + h], rt[h], rt[h].ap, eng="pool")
                    P.load(St['PC'], S['PC'][pd * 2 + h], pcb[h], pcb[h].ap, eng="pool")
                mA, mB, mC = cst['mA%d' % d], cst['mB%d' % d], cst['mC%d' % d]
                for c0 in range(0, nch, NB):
                    nb = min(NB, nch - c0)
                    cs = slice(c0, c0 + nb)
                    for i in range(nb):
                        c = c0 + i
                        k2 = kT2[:, c, :, :].rearrange("p a b -> p (a b)")
                        P.mm(psKK, psKK[:, i, :], kT2, k2, kT, kT[:, c * 64:(c + 1) * 64], True, True)
                        P.mm(psQK, psQK[:, i, :], kT2, k2, qT, qT[:, c * 64:(c + 1) * 64], True, True)
                    gcol = (cols['gc'], bc3(cols['gc'][:, cs], 64))
                    bcol = (cols['be'], bc3(cols['be'][:, cs], 64))
                    P.V("dve", "tensor_tensor", (E1, E1[:, :nb]), gcol, (gcbc, gcbc[:, cs]), ALU.subtract)
                    P.V("dve", "tensor_scalar_min", (E1, E1[:, :nb]), (E1, E1[:, :nb]), 0.0)
                    P.V("act", "activation", (E1, E1[:, :nb]), (E1, E1[:, :nb]), AF.Exp)
                    P.V("dve", "tensor_tensor", (E2, E2[:, :nb]), (gcbc, gcbc[:, cs]), gcol, ALU.subtract)
                    P.V("dve", "tensor_scalar_min", (E2, E2[:, :nb]), (E2, E2[:, :nb]), 0.0)
                    P.V("act", "activation", (E2, E2[:, :nb]), (E2, E2[:, :nb]), AF.Exp)
                    P.V("pool", "tensor_tensor", (tmp, tmp[:, :nb]), (E1, E1[:, :nb]), (mA, mA[:, :nb]), ALU.mult)
                    P.V("pool", "tensor_tensor", (tmp, tmp[:, :nb]), (tmp, tmp[:, :nb]), bcol, ALU.mult)
                    P.V("dve", "tensor_tensor", (A, A[:, :nb]), (tmp, tmp[:, :nb]), (psKK, psKK[:, :nb]), ALU.mult)
                    P.V("pool", "tensor_tensor", (tmp, tmp[:, :nb]), (E2, E2[:, :nb]), (mB, mB[:, :nb]), ALU.mult)
                    P.V("pool", "tensor_tensor", (tmp, tmp[:, :nb]), (tmp, tmp[:, :nb]), (bebc, bebc[:, cs]), ALU.mult)
                    P.V("dve", "tensor_tensor", (B, B[:, :nb]), (tmp, tmp[:, :nb]), (psKK, psKK[:, :nb]), ALU.mult)
                    P.V("pool", "tensor_tensor", (tmp, tmp[:, :nb]), (E2, E2[:, :nb]), (mC, mC[:, :nb]), ALU.mult)
                    P.V("dve", "tensor_tensor", (arb, arb[:, :nb]), (tmp, tmp[:, :nb]), (psQK, psQK[:, :nb]), ALU.mult)
                    P.load(St['ARB'], S['ARB'][pd, :, cs, :], arb, arb[:, :nb], eng="pool")
                    G = inverse_T(P, A, B, cst['ident2'], NB, ib)
                    P.V("dve", "tensor_tensor", (tat, tat[:, :nb]), (G, G[:, :nb]), bcol, ALU.mult)
                    P.load(St['TAT'], S['TAT'][pd, :, cs, :], tat, tat[:, :nb], eng="pool")
                    P.V("pool", "tensor_tensor", (atok, atok[:, :nb]), (ktok, ktok[:, cs]), (cols['ebg'], bc3(cols['ebg'][:, cs], 128)), ALU.mult)
                    P.V("pool", "tensor_tensor", (bp, bp[:, :nb]), (ktok, ktok[:, cs]), (cols['ktl'], bc3(cols['ktl'][:, cs], 128)), ALU.mult)
                    P.load(St['BP'], S['BP'][pd, :, cs, :], bp, bp[:, :nb], eng="pool")
                    for h in range(2):
                        hs = slice(h * 64, h * 64 + 64)
                        for i in range(nb):
                            P.mm(psW[h], psW[h][:, i, :], atok, atok[hs, i, :], G, G[hs, i, :], True, True)
                        P.V("act", "activation", (wt[h], wt[h][:, :nb]), (psW[h], psW[h][:, :nb]), AF.Identity, scale=-1.0)
                        P.load(St['WT'], S['WT'][pd * 2 + h, :, c0 * 64:(c0 + nb) * 64], wt[h], wt[h][:, :nb].rearrange("p a b -> p (a b)"), eng="sp")
        P.barrier(); P.es = old
    return St


def gdn_post(P, D, Tn, nkh, Y_t, zT, zT_t, normg, Wout, consts, yT, yT_t):
    nc = P.nc
    nvh = 2 * nkh; nch = Tn // 64; NV = nvh * 128
    Y = Y_t.ap
    oTd = dram(nc, "g_oTd", [NV, Tn], BF16); oTd_t = T(oTd)
    with ExitStack() as es2:
        old = P.es; P.es = es2
        id2 = P.sb([128, 64], F32)
        P.load(id2, id2.ap, T(consts['ident2']), consts['ident2'][:, 0:64])
        ng = P.sb([128, 1], F32)
        P.load(ng, ng.ap, T(normg), normg)
        eps6 = P.sb([128, 1], F32)
        P.V("dve", "memset", eps6, 1e-6)
        zs = [P.sb([128, Tn], BF16) for _ in range(2)]
        yf = P.sb([128, NB, 128], F32); yb = P.sb([128, NB, 128], F32); sq = P.sb([128, NB, 128], F32)
        ss = P.sb([128, NB], F32)
        psO = [P.ps([128, NB * 64]) for _ in range(2)]
        ob = [P.sb([128, NB * 64], BF16) for _ in range(2)]
        for j in range(nkh):
            for h in range(2):
                P.load(zs[h], zs[h].ap, zT_t, zT[(2 * j + h) * 128:(2 * j + h + 1) * 128, :])
            for c0 in range(0, nch, NB):
                nb = min(NB, nch - c0); cs = slice(c0, c0 + nb)
                P.load(yf, yf[:, :nb], Y_t, Y[2 * j, :, cs, :])
                P.load(yb, yb[:, :nb], Y_t, Y[2 * j + 1, :, cs, :], eng="pool")
                P.V("dve", "tensor_tensor", (yf, yf[:, :nb]), (yf, yf[:, :nb]), (yb, yb[:, :nb]), ALU.add)
                P.V("pool", "tensor_tensor", (sq, sq[:, :nb]), (yf, yf[:, :nb]), (yf, yf[:, :nb]), ALU.mult)
                P.V("dve", "reduce_sum", (ss, ss[:, :nb]), (sq, sq[:, :nb]), AX.X)
                P.V("act", "activation", (ss, ss[:, :nb]), (ss, ss[:, :nb]), AF.Sqrt, bias=(eps6, eps6.ap), scale=1.0 / 128)
                P.V("dve", "reciprocal", (ss, ss[:, :nb]), (ss, ss[:, :nb]))
                P.V("dve", "tensor_tensor", (yf, yf[:, :nb]), (yf, yf[:, :nb]), (ss, bc3(ss[:, :nb], 128)), ALU.mult)
                for h in range(2):
                    hs = slice(h * 64, h * 64 + 64)
                    for i in range(nb):
                        P.mm(psO[h], psO[h][:, i * 64:(i + 1) * 64], yf, yf[hs, i, :], id2, id2[hs, :], True, True)
                    P.V("dve", "scalar_tensor_tensor", (ob[h], ob[h][:, :nb * 64]), (psO[h], psO[h][:, :nb * 64]), (ng, ng.ap),
                        (zs[h], zs[h][:, c0 * 64:(c0 + nb) * 64]), ALU.mult, ALU.mult)
                    P.load(oTd_t, oTd[(2 * j + h) * 128:(2 * j + h + 1) * 128, c0 * 64:(c0 + nb) * 64], ob[h], ob[h][:, :nb * 64], eng="pool")
        P.barrier(); P.es = old
    linear_fm(P, oTd_t, oTd, NV, Tn, T(Wout), Wout, D, [(0, D, yT_t, yT, F32, {})])


def gdn_layer(P, D, Tn, nkh, hT, W, convw, alog, dtb, normg, Wout, consts, yT, yT_t):
    nc = P.nc
    nvh = 2 * nkh; nch = Tn // 64
    qkT, vTb, zT, scT, pcT, toks = gdn_mixer(P, D, Tn, nkh, hT, W, convw, alog, dtb, normg, Wout, consts, yT)
    npr = nkh * 2
    S = dict(WT=dram(nc, "s_WT", [npr * 2, 128, Tn], BF16), RT=dram(nc, "s_RT", [npr * 2, 128, Tn], BF16),
             ARB=dram(nc, "s_ARB", [npr, 128, nch, 64], BF16), TAT=dram(nc, "s_TAT", [npr, 128, nch, 64], BF16),
             BP=dram(nc, "s_BP", [npr, 128, nch, 128], BF16), V=dram(nc, "s_V", [npr, 128, nch, 128], BF16),
             PC=dram(nc, "s_PC", [npr * 2, 128, nch], F32))
    St = gdn_chunks(P, Tn, nkh, qkT, vTb, scT, pcT, toks, consts, S)
    Y_t = T(dram(nc, "s_Y", [npr, 128, nch, 128], F32))
    scan_core(P, npr, nch, 128, 128, St['WT'], St['RT'], St['ARB'], St['TAT'], None, St['BP'], None, St['V'], St['PC'], Y_t,
              has_k=False, revs=[(i % 2 == 1) for i in range(npr)])
    gdn_post(P, D, Tn, nkh, Y_t, zT, toks[2], normg, Wout, consts, yT, yT_t)


def make_consts():
    import ml_dtypes
    i64 = np.eye(64, dtype=np.float32)
    id2 = np.tile(np.concatenate([i64, i64], 0), (1, NB))
    t = np.arange(64)
    lowS = (t[:, None] > t[None, :]).astype(np.float32)
    upS = (t[:, None] < t[None, :]).astype(np.float32)
    upI = (t[:, None] <= t[None, :]).astype(np.float32)
    lowI = (t[:, None] >= t[None, :]).astype(np.float32)
    st = lambda m: np.ascontiguousarray(np.tile(np.concatenate([m, m], 0), (1, NB)))
    bo = np.zeros((128, 128), np.float32); bo[:64, :64] = 1; bo[64:, 64:] = 1
    return dict(bones=bo, ident2=id2, ident128=np.eye(128).astype(ml_dtypes.bfloat16), ones128=np.ones((128, 128), np.float32),
                mA0=st(-lowS), mB0=st(-upS), mC0=st(upI), mA1=st(-upS), mB1=st(-lowS), mC1=st(lowI))

import numpy as np

DECAY = 0.6065306597126334

def rwkv_layer(P, D, Tn, nhp, hT_t, W, mu, W2, w0, A2, a0, G2, kk_, ka_, rk_, gng, gnb, consts, yoT_t, yo_row0):
    nc = P.nc
    nch = Tn // 64
    NH = nhp * 128
    c_r, c_k, c_v, c_wd, c_ad, c_gd, NR = 0, NH, 2 * NH, 3 * NH, 3 * NH + 192, 3 * NH + 384, 3 * NH + 640
    pT = dram(nc, "r_pT", [NR, Tn], F32); pT_t = T(pT)
    linear_fm(P, hT_t, hT_t.ap, D, Tn, T(W), W, NR, [(0, NR, pT_t, pT, F32, {})])
    zT = dram(nc, "r_zT", [3 * NH, Tn], F32); zT_t = T(zT)
    lwd = dram(nc, "r_lwd", [2, 96, Tn], BF16); lwd_t = T(lwd)
    lad = dram(nc, "r_lad", [2, 96, Tn], BF16); lad_t = T(lad)
    lgd = dram(nc, "r_lgd", [256, Tn], BF16); lgd_t = T(lgd)
    mu_t = T(mu)
    with ExitStack() as es2:
        old = P.es; P.es = es2
        xp = [P.sb([128, Tn + 2], F32) for _ in range(2)]
        s_ = [P.sb([128, Tn], F32) for _ in range(2)]
        z_ = [P.sb([128, Tn], F32) for _ in range(2)]
        zb = [P.sb([128, Tn], BF16) for _ in range(2)]
        m_ = [P.sb([128, 3], F32) for _ in range(2)]
        for b_ in xp:
            P.V("pool", "memset", (b_, b_[:, 0:1]), 0.0)
            P.V("pool", "memset", (b_, b_[:, Tn + 1:Tn + 2]), 0.0)
        segs = [(i * 128, 128, ('z', i * 128)) for i in range(3 * nhp)]
        segs += [(c_wd, 96, ('wd', 0)), (c_wd + 96, 96, ('wd', 1)), (c_ad, 96, ('ad', 0)), (c_ad + 96, 96, ('ad', 1)),
                 (c_gd, 128, ('gd', 0)), (c_gd + 128, 128, ('gd', 1))]
        for si, (r0, n, dst) in enumerate(segs):
            x_, sm, z, zbb, m = xp[si % 2], s_[si % 2], z_[si % 2], zb[si % 2], m_[si % 2]
            P.load(x_, x_[:n, 1:Tn + 1], pT_t, pT[r0:r0 + n, :])
            P.load(m, m[:n, 0:1], mu_t, mu[r0:r0 + n, :], eng="pool")
            P.V("dve", "tensor_scalar", (m, m[:n, 1:2]), (m, m[:n, 0:1]), -1.0, 1.0, ALU.mult, ALU.add)
            P.V("dve", "tensor_scalar_mul", (m, m[:n, 2:3]), (m, m[:n, 0:1]), 0.5)
            P.V("pool", "tensor_tensor", (sm, sm[:n]), (x_, x_[:n, 0:Tn]), (x_, x_[:n, 2:Tn + 2]), ALU.add)
            P.V("act", "activation", (z, z[:n]), (x_, x_[:n, 1:Tn + 1]), AF.Identity, scale=(m, m[:n, 1:2]))
            P.V("dve", "scalar_tensor_tensor", (z, z[:n]), (sm, sm[:n]), (m, m[:n, 2:3]), (z, z[:n]), ALU.mult, ALU.add)
            kind, idx = dst
            if kind == 'z':
                P.load(zT_t, zT[idx:idx + 128, :], z, z.ap, eng="pool")
            elif kind == 'wd':
                P.V("act", "activation", (zbb, zbb[:n]), (z, z[:n]), AF.Tanh)
                P.load(lwd_t, lwd[idx], zbb, zbb[:n], eng="pool")
            elif kind == 'ad':
                P.V("act", "copy", (zbb, zbb[:n]), (z, z[:n]))
                P.load(lad_t, lad[idx], zbb, zbb[:n], eng="pool")
            else:
                P.V("act", "activation", (zbb, zbb[:n]), (z, z[:n]), AF.Sigmoid)
                P.load(lgd_t, lgd[idx * 128:(idx + 1) * 128, :], zbb, zbb[:n], eng="pool")
        P.barrier(); P.es = old
    sgT = dram(nc, "r_sgT", [2, NH, Tn], F32); sgT_t = T(sgT)
    aT = dram(nc, "r_aT", [2, NH, Tn], F32); aT_t = T(aT)
    gT = dram(nc, "r_gT", [NH, Tn], BF16); gT_t = T(gT)
    with ExitStack() as es2:
        old = P.es; P.es = es2
        bw = [[P.sb([128, 1], F32) for _ in range(nhp)] for _ in range(4)]
        P.es = old
        for d in range(2):
            for i in range(nhp):
                P.load(bw[d][i], bw[d][i].ap, T(w0), w0[d, i * 128:(i + 1) * 128, :])
                P.load(bw[2 + d][i], bw[2 + d][i].ap, T(a0), a0[d, i * 128:(i + 1) * 128, :])
        for d in range(2):
            linear_fm(P, lwd_t, lwd[d], 96, Tn, T(W2), W2[d], NH,
                      [(i * 128, (i + 1) * 128, sgT_t, sgT[d, i * 128:(i + 1) * 128, :], F32, dict(func=AF.Sigmoid, bias=(bw[d][i], bw[d][i].ap))) for i in range(nhp)])
            linear_fm(P, lad_t, lad[d], 96, Tn, T(A2), A2[d], NH,
                      [(i * 128, (i + 1) * 128, aT_t, aT[d, i * 128:(i + 1) * 128, :], F32, dict(func=AF.Sigmoid, bias=(bw[2 + d][i], bw[2 + d][i].ap))) for i in range(nhp)])
        linear_fm(P, lgd_t, lgd, 256, Tn, T(G2), G2, NH, [(0, NH, gT_t, gT, BF16, {})])
        P.barrier()
    return zT_t, sgT_t, aT_t, gT_t


def rwkv_chunks(P, Tn, nhp, zT_t, sgT_t, aT_t, kk_, ka_, consts, S, TBK=1024):
    nc = P.nc
    nch = Tn // 64; NH = nhp * 128
    TBK = min(TBK, Tn); ncb = TBK // 64
    zT, sgT, aT = zT_t.ap, sgT_t.ap, aT_t.ap
    St = {k: T(v) for k, v in S.items()}
    with ExitStack() as es2:
        old = P.es; P.es = es2
        cst = {}
        for k in ('ident2', 'mA0', 'mB0', 'mC0', 'mA1', 'mB1', 'mC1'):
            cst[k] = P.sb([128, NB, 64], F32)
            P.load(cst[k], cst[k].ap.rearrange("p a b -> p (a b)"), T(consts[k]), consts[k])
        bones = P.sb([128, 128], F32)
        P.load(bones, bones.ap, T(consts['bones']), consts['bones'])
        eps6 = P.sb([128, 1], F32); P.V("dve", "memset", eps6, 1e-6)
        id2 = cst['ident2']
        F = lambda: P.sb([128, TBK], F32)
        r, k, v, sg, a = F(), F(), F(), F(), F()
        kk, kd, b, lw, lc2, t1, ea, en, ep, el = F(), F(), F(), F(), F(), F(), F(), F(), F(), F()
        aT_, bT_, kT_, rT_, bpT, kpT = F(), F(), F(), F(), F(), F()
        rtb = P.sb([128, TBK], BF16)
        cols = P.sb([128, 2], F32)
        ps512 = P.ps([128, 512])
        psT = [P.ps([128, NB, 64]) for _ in range(2)]
        psW = P.ps([128, NB, 64])
        ib = inv_bufs(P, NB)
        A = P.sb([128, NB, 64], F32); B = P.sb([128, NB, 64], F32); AK = P.sb([128, NB, 64], F32)
        arb = P.sb([128, NB, 64], BF16); ark = P.sb([128, NB, 64], BF16); tat = P.sb([128, NB, 64], BF16)
        atok = P.sb([128, NB, 64], F32); bp = P.sb([128, NB, 64], BF16); kp = P.sb([128, NB, 64], BF16); vt = P.sb([128, NB, 64], BF16)
        wt = P.sb([128, NB, 64], BF16)
        pc = P.sb([128, ncb], F32)
        for i in range(nhp):
            P.load(cols, cols[:, 0:1], T(kk_), kk_[i * 128:(i + 1) * 128, :])
            P.load(cols, cols[:, 1:2], T(ka_), ka_[i * 128:(i + 1) * 128, :])
            for t0 in range(0, Tn, TBK):
                ts_ = slice(t0, t0 + TBK)
                P.load(r, r.ap, zT_t, zT[i * 128:(i + 1) * 128, ts_])
                P.load(k, k.ap, zT_t, zT[NH + i * 128:NH + (i + 1) * 128, ts_], eng="pool")
                P.load(v, v.ap, zT_t, zT[2 * NH + i * 128:2 * NH + (i + 1) * 128, ts_])
                P.V("dve", "tensor_scalar", kk, k, (cols, cols[:, 0:1]), None, ALU.mult)
                P.V("act", "activation", t1, kk, AF.Square)
                for s0 in range(0, TBK, 512):
                    sw = min(512, TBK - s0)
                    P.mm(ps512, ps512[:, :sw], bones, bones.ap, t1, t1[:, s0:s0 + sw], True, True)
                    P.V("act", "activation", (ea, ea[:, s0:s0 + sw]), (ps512, ps512[:, :sw]), AF.Sqrt, bias=(eps6, eps6.ap))
                P.V("dve", "reciprocal", ea, ea)
                P.V("dve", "tensor_tensor", kk, kk, ea, ALU.mult)
                for d in range(2):
                    pd = i * 2 + d
                    P.load(sg, sg.ap, sgT_t, sgT[d, i * 128:(i + 1) * 128, ts_])
                    P.load(a, a.ap, aT_t, aT[d, i * 128:(i + 1) * 128, ts_], eng="pool")
                    P.V("dve", "tensor_scalar", t1, a, -1.0, (cols, cols[:, 1:2]), ALU.add, ALU.mult)
                    P.V("dve", "scalar_tensor_tensor", kd, t1, 1.0, k, ALU.add, ALU.mult)
                    P.V("pool", "tensor_tensor", b, kk, a, ALU.mult)
                    P.V("act", "activation", lw, sg, AF.Identity, scale=-DECAY)
                    P.V("pool", "tensor_copy", t1, lw)
                    lc = cumsum64(P, t1, 128, ncb, reverse=(d == 1), y=lc2)
                    lc3 = lc.ap.rearrange("p (c t) -> p c t", t=64)
                    last = lc3[:, :, 63:64] if d == 0 else lc3[:, :, 0:1]
                    P.V("dve", "tensor_tensor", ea, lc, lw, ALU.subtract)
                    P.V("act", "activation", ea, ea, AF.Exp)
                    P.V("act", "activation", en, lc, AF.Exp, scale=-1.0)
                    P.V("act", "activation", ep, lc, AF.Exp)
                    P.V("dve", "tensor_tensor", (el, el.ap.rearrange("p (c t) -> p c t", t=64)), (lc, last.to_broadcast([128, ncb, 64])), (lc, lc3), ALU.subtract)
                    P.V("act", "activation", el, el, AF.Exp)
                    P.V("act", "activation", (pc, pc.ap.unsqueeze(2)), (lc, last), AF.Exp)
                    P.load(St['PC'], S['PC'][pd, :, t0 // 64:t0 // 64 + ncb], pc, pc.ap, eng="pool")
                    P.V("dve", "scalar_tensor_tensor", aT_, kk, -1.0, ea, ALU.mult, ALU.mult)
                    P.V("pool", "tensor_tensor", bT_, b, en, ALU.mult)
                    P.V("dve", "tensor_tensor", kT_, kd, en, ALU.mult)
                    P.V("pool", "tensor_tensor", rT_, r, ep, ALU.mult)
                    P.V("dve", "tensor_tensor", bpT, b, el, ALU.mult)
                    P.V("pool", "tensor_tensor", kpT, kd, el, ALU.mult)
                    P.V("act", "copy", rtb, rT_)
                    P.load(St['RT'], S['RT'][pd, :, ts_], rtb, rtb.ap, eng="pool")
                    mA, mB, mC = cst['mA%d' % d], cst['mB%d' % d], cst['mC%d' % d]
                    for cb in range(0, ncb, NB):
                        cs = slice(t0 // 64 + cb, t0 // 64 + cb + NB)
                        def gram(ps, lt, rt_):
                            for ii in range(NB):
                                c = cb + ii
                                for h in range(2):
                                    hs = slice(h * 64, h * 64 + 64)
                                    P.mm(ps, ps[hs, ii, :], lt, lt[hs, c * 64:(c + 1) * 64], rt_, rt_[hs, c * 64:(c + 1) * 64], True, True)
                        def tpose(ps, src):
                            for ii in range(NB):
                                c = cb + ii
                                for h in range(2):
                                    hs = slice(h * 64, h * 64 + 64)
                                    P.mm(ps, ps[hs, ii, :], src, src[hs, c * 64:(c + 1) * 64], id2, id2[hs, 0, :], True, True)
                        gram(psT[0], aT_, bT_)
                        P.V("dve", "scalar_tensor_tensor", A, psT[0], -1.0, mA, ALU.mult, ALU.mult)
                        gram(psT[1], bT_, aT_)
                        P.V("dve", "scalar_tensor_tensor", B, psT[1], -1.0, mB, ALU.mult, ALU.mult)
                        gram(psT[0], aT_, kT_)
                        P.V("dve", "scalar_tensor_tensor", AK, psT[0], -1.0, mA, ALU.mult, ALU.mult)
                        gram(psT[1], bT_, rT_)
                        P.V("dve", "tensor_tensor", arb, psT[1], mC, ALU.mult)
                        gram(psT[0], kT_, rT_)
                        P.V("dve", "tensor_tensor", ark, psT[0], mC, ALU.mult)
                        P.load(St['ARB'], S['ARB'][pd, :, cs, :], arb, arb.ap, eng="pool")
                        P.load(St['ARK'], S['ARK'][pd, :, cs, :], ark, ark.ap, eng="pool")
                        tpose(psT[1], aT_); P.V("act", "copy", atok, psT[1])
                        tpose(psT[0], bpT); P.V("act", "copy", bp, psT[0])
                        tpose(psT[1], kpT); P.V("act", "copy", kp, psT[1])
                        P.load(St['BP'], S['BP'][pd, :, cs, :], bp, bp.ap, eng="pool")
                        P.load(St['KP'], S['KP'][pd, :, cs, :], kp, kp.ap, eng="pool")
                        if d == 0:
                            tpose(psT[0], v); P.V("act", "copy", vt, psT[0])
                            P.load(St['V'], S['V'][pd, :, cs, :], vt, vt.ap, eng="pool")
                            P.load(St['V'], S['V'][pd + 1, :, cs, :], vt, vt.ap, eng="pool")
                        G = inverse_T(P, A, B, id2, NB, ib)
                        for ii in range(NB):
                            for h in range(2):
                                hs = slice(h * 64, h * 64 + 64)
                                P.mm(psT[0], psT[0][hs, ii, :], AK, AK[hs, ii, :], G, G[hs, ii, :], True, True)
                                P.mm(psW, psW[hs, ii, :], atok, atok[hs, ii, :], G, G[hs, ii, :], True, True)
                        P.V("act", "copy", tat, psT[0])
                        P.V("act", "copy", wt, psW)
                        P.load(St['TAT'], S['TAT'][pd, :, cs, :], tat, tat.ap, eng="pool")
                        P.load(St['WT'], S['WT'][pd, :, t0 + cb * 64:t0 + (cb + NB) * 64], wt, wt.ap.rearrange("p a b -> p (a b)"), eng="sp")
        P.barrier(); P.es = old
    return St


def rwkv_post(P, Tn, nhp, Y_t, zT_t, aT_t, gT_t, ka_, rk_, gng, gnb, consts, yoT_t, yo_row0, TBK=1024):
    nc = P.nc
    nch = Tn // 64; NH = nhp * 128
    TBK = min(TBK, Tn); ncb = TBK // 64
    Y, zT, aT, gT, yoT = Y_t.ap, zT_t.ap, aT_t.ap, gT_t.ap, yoT_t.ap
    with ExitStack() as es2:
        old = P.es; P.es = es2
        id2 = P.sb([128, 64], F32)
        P.load(id2, id2.ap, T(consts['ident2']), consts['ident2'][:, 0:64])
        bones = P.sb([128, 128], F32)
        P.load(bones, bones.ap, T(consts['bones']), consts['bones'])
        eps = P.sb([128, 1], F32); P.V("dve", "memset", eps, 64e-5)
        cols = P.sb([128, 4], F32)
        F = lambda: P.sb([128, TBK], F32)
        r, k, v, a0, a1, t1, yn = F(), F(), F(), F(), F(), F(), F()
        g = P.sb([128, TBK], BF16); ob = P.sb([128, TBK], BF16)
        yf = P.sb([128, ncb, 64], F32); yb = P.sb([128, ncb, 64], F32); sq = P.sb([128, ncb, 64], F32)
        mean = P.sb([128, ncb], F32); ex2 = P.sb([128, ncb], F32); var = P.sb([128, ncb], F32)
        psO = P.ps([128, 512]); psB = P.ps([128, 512])
        for i in range(nhp):
            for j, src in enumerate((ka_, rk_, gng, gnb)):
                P.load(cols, cols[:, j:j + 1], T(src), src[i * 128:(i + 1) * 128, :])
            for t0 in range(0, Tn, TBK):
                ts_ = slice(t0, t0 + TBK); cs = slice(t0 // 64, t0 // 64 + ncb)
                P.load(yf, yf.ap, Y_t, Y[2 * i, :, cs, :])
                P.load(yb, yb.ap, Y_t, Y[2 * i + 1, :, cs, :], eng="pool")
                P.load(r, r.ap, zT_t, zT[i * 128:(i + 1) * 128, ts_])
                P.load(k, k.ap, zT_t, zT[NH + i * 128:NH + (i + 1) * 128, ts_], eng="pool")
                P.load(v, v.ap, zT_t, zT[2 * NH + i * 128:2 * NH + (i + 1) * 128, ts_])
                P.load(a0, a0.ap, aT_t, aT[0, i * 128:(i + 1) * 128, ts_], eng="pool")
                P.load(a1, a1.ap, aT_t, aT[1, i * 128:(i + 1) * 128, ts_])
                P.load(g, g.ap, gT_t, gT[i * 128:(i + 1) * 128, ts_], eng="pool")
                P.V("dve", "tensor_tensor", yf, yf, yb, ALU.add)
                P.V("pool", "tensor_tensor", sq, yf, yf, ALU.mult)
                P.V("dve", "reduce_sum", mean, yf, AX.X)
                P.V("dve", "reduce_sum", ex2, sq, AX.X)
                P.V("dve", "tensor_scalar_mul", mean, mean, 1.0 / 64)
                P.V("dve", "tensor_tensor", var, mean, mean, ALU.mult)
                P.V("dve", "scalar_tensor_tensor", var, ex2, 1.0 / 64, var, ALU.mult, ALU.subtract)
                P.V("act", "activation", var, var, AF.Sqrt, bias=(eps, eps.ap))
                P.V("dve", "reciprocal", var, var)
                P.V("dve", "tensor_tensor", yf, yf, (mean, bc3(mean.ap, 64)), ALU.subtract)
                P.V("dve", "tensor_tensor", yf, yf, (var, bc3(var.ap, 64)), ALU.mult)
                P.V("pool", "tensor_tensor", t1, a0, a1, ALU.add)
                P.V("dve", "tensor_scalar", t1, t1, -2.0, (cols, cols[:, 0:1]), ALU.add, ALU.mult)
                P.V("dve", "scalar_tensor_tensor", t1, t1, 2.0, k, ALU.add, ALU.mult)
                P.V("dve", "scalar_tensor_tensor", t1, t1, (cols, cols[:, 1:2]), r, ALU.mult, ALU.mult)
                for s0 in range(0, TBK, 512):
                    sw = min(512, TBK - s0)
                    n8 = sw // 64
                    for ii in range(n8):
                        c = s0 // 64 + ii
                        for h in range(2):
                            hs = slice(h * 64, h * 64 + 64)
                            P.mm(psO, psO[hs, ii * 64:(ii + 1) * 64], yf, yf[hs, c, :], id2, id2[hs, :], True, True)
                    P.V("act", "activation", (yn, yn[:, s0:s0 + sw]), (psO, psO[:, :sw]), AF.Identity,
                        scale=(cols, cols[:, 2:3]), bias=(cols, cols[:, 3:4]))
                    P.mm(psB, psB[:, :sw], bones, bones.ap, t1, t1[:, s0:s0 + sw], True, True)
                    P.V("dve", "tensor_tensor", (k, k[:, s0:s0 + sw]), (psB, psB[:, :sw]), (v, v[:, s0:s0 + sw]), ALU.mult)
                P.V("dve", "tensor_tensor", yn, yn, k, ALU.add)
                P.V("dve", "tensor_tensor", ob, yn, g, ALU.mult)
                P.load(yoT_t, yoT[yo_row0 + i * 128:yo_row0 + (i + 1) * 128, ts_], ob, ob.ap, eng="pool")
        P.barrier(); P.es = old


def rwkv_full(P, D, Tn, nhp, hT_t, W, mu, W2, w0, A2, a0, G2, kk_, ka_, rk_, gng, gnb, consts, yoT_t, yo_row0):
    nc = P.nc
    nch = Tn // 64
    zT_t, sgT_t, aT_t, gT_t = rwkv_layer(P, D, Tn, nhp, hT_t, W, mu, W2, w0, A2, a0, G2, kk_, ka_, rk_, gng, gnb, consts, yoT_t, yo_row0)
    npr = nhp * 2
    S = dict(WT=dram(nc, "rs_WT", [npr, 128, Tn], BF16), RT=dram(nc, "rs_RT", [npr, 128, Tn], BF16),
             ARB=dram(nc, "rs_ARB", [npr, 128, nch, 64], BF16), TAT=dram(nc, "rs_TAT", [npr, 128, nch, 64], BF16),
             ARK=dram(nc, "rs_ARK", [npr, 128, nch, 64], BF16),
             BP=dram(nc, "rs_BP", [npr, 128, nch, 64], BF16), KP=dram(nc, "rs_KP", [npr, 128, nch, 64], BF16),
             V=dram(nc, "rs_V", [npr, 128, nch, 64], BF16), PC=dram(nc, "rs_PC", [npr, 128, nch], F32))
    St = rwkv_chunks(P, Tn, nhp, zT_t, sgT_t, aT_t, kk_, ka_, consts, S)
    Y_t = T(dram(nc, "rs_Y", [npr, 128, nch, 64], F32))
    scan_core(P, npr, nch, 64, 64, St['WT'], St['RT'], St['ARB'], St['TAT'], St['ARK'], St['BP'], St['KP'], St['V'], St['PC'], Y_t,
              CH=min(nch, 32), has_k=True, revs=[(i % 2 == 1) for i in range(npr)])
    rwkv_post(P, Tn, nhp, Y_t, zT_t, aT_t, gT_t, ka_, rk_, gng, gnb, consts, yoT_t, yo_row0)

ALPHA = 8 ** 0.25
PI = 3.141592653589793


def mod_phase(P, D, Tn, xT_t, sc, sh, hT_t):
    xT, hT = xT_t.ap, hT_t.ap
    with ExitStack() as es2:
        old = P.es; P.es = es2
        KC = D // 128
        cs = P.sb([128, KC, 2], F32)
        for kc in range(KC):
            P.load(cs, cs[:, kc, 0:1], T(sc), sc[kc * 128:(kc + 1) * 128, :])
            P.load(cs, cs[:, kc, 1:2], T(sh), sh[kc * 128:(kc + 1) * 128, :], eng="pool")
        P.V("dve", "tensor_scalar_add", (cs, cs[:, :, 0:1]), (cs, cs[:, :, 0:1]), 1.0)
        xb = [P.sb([128, Tn], F32) for _ in range(2)]
        hb = [P.sb([128, Tn], BF16) for _ in range(2)]
        for kc in range(KC):
            x_, h_ = xb[kc % 2], hb[kc % 2]
            P.load(x_, x_.ap, xT_t, xT[kc * 128:(kc + 1) * 128, :])
            P.V("act", "activation", h_, x_, AF.Identity, scale=(cs, cs[:, kc, 0:1]), bias=(cs, cs[:, kc, 1:2]))
            P.load(hT_t, hT[kc * 128:(kc + 1) * 128, :], h_, h_.ap, eng="pool")
        P.barrier(); P.es = old


def ln_phase(P, D, Tn, xT_t, ya_t, yb_t, gate, lng, lnb, ones128, outT_t):
    xT, ya, yb, outT = xT_t.ap, ya_t.ap, yb_t.ap, outT_t.ap
    KC = D // 128
    with ExitStack() as es2:
        old = P.es; P.es = es2
        cs = P.sb([128, KC, 3], F32)
        for kc in range(KC):
            for j, src in enumerate((gate, lng, lnb)):
                P.load(cs, cs[:, kc, j:j + 1], T(src), src[kc * 128:(kc + 1) * 128, :], eng=("sp" if j % 2 == 0 else "pool"))
        ones = P.sb([128, 128], F32)
        P.load(ones, ones.ap, T(ones128), ones128)
        eps = P.sb([128, 1], F32); P.V("dve", "memset", eps, 1e-5)
        z = P.sb([128, KC, 512], F32)
        a_ = [P.sb([128, 512], F32) for _ in range(2)]; b_ = [P.sb([128, 512], F32) for _ in range(2)]
        x_ = [P.sb([128, 512], F32) for _ in range(2)]
        sq = [P.sb([128, 512], F32) for _ in range(2)]
        ps_s, ps_q = P.ps([128, 512]), P.ps([128, 512])
        mean = P.sb([128, 512], F32); rstd = P.sb([128, 512], F32); t = P.sb([128, 512], F32)
        o_ = [P.sb([128, 512], F32) for _ in range(2)]
        for t0 in range(0, Tn, 512):
            ts_ = slice(t0, t0 + 512)
            for kc in range(KC):
                a, b, x, s = a_[kc % 2], b_[kc % 2], x_[kc % 2], sq[kc % 2]
                rs = slice(kc * 128, (kc + 1) * 128)
                P.load(a, a.ap, ya_t, ya[rs, ts_]); P.load(b, b.ap, yb_t, yb[rs, ts_], eng="pool"); P.load(x, x.ap, xT_t, xT[rs, ts_])
                P.V("pool", "tensor_tensor", a, a, b, ALU.add)
                P.V("dve", "tensor_scalar", a, a, (cs, cs[:, kc, 0:1]), None, ALU.mult)
                P.V("dve", "scalar_tensor_tensor", (z, z[:, kc, :]), x, ALPHA, a, ALU.mult, ALU.add)
                P.V("act", "activation", s, (z, z[:, kc, :]), AF.Square)
                P.mm(ps_s, ps_s.ap, ones, ones.ap, z, z[:, kc, :], kc == 0, kc == KC - 1)
                P.mm(ps_q, ps_q.ap, ones, ones.ap, s, s.ap, kc == 0, kc == KC - 1)
            P.V("dve", "tensor_scalar_mul", mean, ps_s, 1.0 / D)
            P.V("dve", "tensor_tensor", t, mean, mean, ALU.mult)
            P.V("dve", "scalar_tensor_tensor", rstd, ps_q, 1.0 / D, t, ALU.mult, ALU.subtract)
            P.V("act", "activation", rstd, rstd, AF.Sqrt, bias=(eps, eps.ap))
            P.V("dve", "reciprocal", rstd, rstd)
            for kc in range(KC):
                o = o_[kc % 2]
                P.V("dve", "tensor_tensor", t, (z, z[:, kc, :]), mean, ALU.subtract)
                P.V("pool", "tensor_tensor", t, t, rstd, ALU.mult)
                P.V("act", "activation", o, t, AF.Identity, scale=(cs, cs[:, kc, 1:2]), bias=(cs, cs[:, kc, 2:3]))
                P.load(outT_t, outT[kc * 128:(kc + 1) * 128, ts_], o, o.ap, eng="pool")
        P.barrier(); P.es = old


def moe_phase(P, D, Tn, hT_t, Wr, Wg, Wu, Wd, ones16, ne, e0, ymT_t):
    nc = P.nc
    FF = 1024; cap = 2 * Tn // 16
    hT = hT_t.ap
    lgT = dram(nc, "m_lgT", [16, Tn], F32); lgT_t = T(lgT)
    linear_fm(P, hT_t, hT, D, Tn, T(Wr), Wr, 16, [(0, 16, lgT_t, lgT, F32, dict(func=AF.Exp))])
    gmT = dram(nc, "m_gmT", [16, Tn], F32); gmT_t = T(gmT)
    with ExitStack() as es2:
        old = P.es; P.es = es2
        e = P.sb([16, Tn], F32); aff = P.sb([16, Tn], F32); wk = P.sb([16, Tn], F32); o16 = P.sb([16, 16], F32)
        mx = P.sb([16, 8], F32); ps = P.ps([16, 512])
        P.load(e, e.ap, lgT_t, lgT); P.load(o16, o16.ap, T(ones16), ones16)
        for s0 in range(0, Tn, 512):
            P.mm(ps, ps.ap, o16, o16.ap, e, e[:, s0:s0 + 512], True, True)
            P.V("dve", "reciprocal", (aff, aff[:, s0:s0 + 512]), ps)
        P.V("dve", "tensor_tensor", aff, aff, e, ALU.mult)
        cur = aff
        for it in range(cap // 8):
            P.V("dve", "max", mx, cur)
            if it < cap // 8 - 1:
                P.V("dve", "match_replace", wk, mx, cur, -1.0)
                cur = wk
        P.V("dve", "tensor_tensor", wk, aff, (mx, mx[:, 7:8].to_broadcast([16, Tn])), ALU.is_ge)
        P.V("dve", "tensor_tensor", wk, wk, aff, ALU.mult)
        P.load(gmT_t, gmT, wk, wk.ap)
        P.barrier(); P.es = old
    hid = dram(nc, "m_hid", [ne * FF, Tn], BF16); hid_t = T(hid)
    TB = 2048 if Tn >= 2048 else Tn
    KC = D // 128
    with ExitStack() as es2:
        old = P.es; P.es = es2
        xs = P.sb([128, KC, TB], BF16)
        gm = P.sb([128, TB], F32)
        wf = [P.sb([128, KC, 128], F32) for _ in range(2)]
        wgb = [P.sb([128, KC, 128], BF16) for _ in range(2)]; wub = [P.sb([128, KC, 128], BF16) for _ in range(2)]
        psg = [P.ps([128, 512]) for _ in range(2)]; psu = [P.ps([128, 512]) for _ in range(2)]
        sg = [P.sb([128, 512], F32) for _ in range(2)]
        ob = [P.sb([128, TB], BF16) for _ in range(2)]
        Wg_t, Wu_t = T(Wg), T(Wu)
        wi = 0; pi = 0
        for t0 in range(0, Tn, TB):
            for kc in range(KC):
                P.load(xs, xs[:, kc, :], hT_t, hT[kc * 128:(kc + 1) * 128, t0:t0 + TB], eng=("sp" if kc % 2 == 0 else "pool"))
            for ei in range(ne):
                P.load(gm, gm.ap, gmT_t, gmT[e0 + ei:e0 + ei + 1, t0:t0 + TB].to_broadcast([128, TB]))
                for fc in range(FF // 128):
                    f_, g_, u_ = wf[wi % 2], wgb[wi % 2], wub[wi % 2]; o = ob[wi % 2]; wi += 1
                    for kc in range(KC):
                        P.load(f_, f_[:, kc, :], Wg_t, Wg[ei, kc * 128:(kc + 1) * 128, fc * 128:(fc + 1) * 128])
                    P.V("pool", "tensor_copy", g_, f_)
                    for kc in range(KC):
                        P.load(f_, f_[:, kc, :], Wu_t, Wu[ei, kc * 128:(kc + 1) * 128, fc * 128:(fc + 1) * 128])
                    P.V("pool", "tensor_copy", u_, f_)
                    for s0 in range(0, TB, 512):
                        pg, pu, s_ = psg[pi % 2], psu[pi % 2], sg[pi % 2]; pi += 1
                        for kc in range(KC):
                            P.mm(pg, pg.ap, g_, g_[:, kc, :], xs, xs[:, kc, s0:s0 + 512], kc == 0, kc == KC - 1)
                        for kc in range(KC):
                            P.mm(pu, pu.ap, u_, u_[:, kc, :], xs, xs[:, kc, s0:s0 + 512], kc == 0, kc == KC - 1)
                        P.V("act", "activation", s_, pg, AF.Silu)
                        P.V("dve", "tensor_tensor", s_, s_, pu, ALU.mult)
                        P.V("pool", "tensor_tensor", (o, o[:, s0:s0 + 512]), s_, (gm, gm[:, s0:s0 + 512]), ALU.mult)
                    P.load(hid_t, hid[ei * FF + fc * 128:ei * FF + (fc + 1) * 128, t0:t0 + TB], o, o.ap, eng="pool")
        P.barrier(); P.es = old
    linear_fm(P, hid_t, hid, ne * FF, Tn, T(Wd), Wd, D, [(0, D, ymT_t, ymT_t.ap, F32, {})], TB=512)


def mla_phase(P, D, Tn, nh, hT_t, Wc, qn, kvn, Wq, Wkv, posf, ropec, consts, yoT_t, yo_row0):
    nc = P.nc
    cT = dram(nc, "a_cT", [2176, Tn], F32); cT_t = T(cT)
    linear_fm(P, hT_t, hT_t.ap, D, Tn, T(Wc), Wc, 2176, [(0, 2176, cT_t, cT, F32, {})])
    cqn = dram(nc, "a_cqn", [1536, Tn], BF16); cqn_t = T(cqn)
    ckn = dram(nc, "a_ckn", [512, Tn], BF16); ckn_t = T(ckn)
    krT = dram(nc, "a_krT", [64, Tn], BF16); krT_t = T(krT)
    tab = dram(nc, "a_tab", [2, 64, Tn], F32); tab_t = T(tab)
    with ExitStack() as es2:
        old = P.es; P.es = es2
        ones = P.sb([128, 128], F32); P.load(ones, ones.ap, T(consts['ones128']), consts['ones128'])
        eps = P.sb([128, 1], F32); P.V("dve", "memset", eps, 1e-6)
        rc = P.sb([64, 3], F32); P.load(rc, rc.ap, T(ropec), ropec)
        pos = P.sb([64, Tn], F32); ang = P.sb([64, Tn], F32); cs_ = P.sb([64, Tn], F32); sn_ = P.sb([64, Tn], F32)
        posi = P.sb([64, Tn], I32)
        P.load(posi, posi.ap, T(posf), posf.to_broadcast([64, Tn]))
        P.V("dve", "tensor_copy", pos, posi)
        P.V("dve", "tensor_scalar", ang, pos, (rc, rc[:, 0:1]), None, ALU.mult)
        def _sinred(dst, off):
            P.V("dve", "tensor_scalar_add", dst, ang, off)
            P.V("dve", "tensor_scalar_mul", pos, dst, 1.0 / (2 * PI))
            P.V("dve", "tensor_copy", posi, pos)
            P.V("dve", "tensor_copy", pos, posi)
            P.V("dve", "scalar_tensor_tensor", dst, pos, -2 * PI, dst, ALU.mult, ALU.add)
            P.V("dve", "tensor_single_scalar", pos, dst, PI, ALU.is_gt)
            P.V("dve", "scalar_tensor_tensor", dst, pos, -2 * PI, dst, ALU.mult, ALU.add)
            P.V("act", "activation", dst, dst, AF.Sin)
        _sinred(sn_, 0.0)
        _sinred(cs_, 0.5 * PI)
        P.V("dve", "tensor_scalar", sn_, sn_, (rc, rc[:, 1:2]), None, ALU.mult)
        P.load(tab_t, tab[0], cs_, cs_.ap); P.load(tab_t, tab[1], sn_, sn_.ap, eng="pool")
        k1 = P.sb([64, Tn], F32); k2 = P.sb([64, Tn], F32); kb = P.sb([64, Tn], BF16)
        P.load(k1, k1.ap, cT_t, cT[2048:2112, :]); P.load(k2, k2.ap, cT_t, cT[2112:2176, :], eng="pool")
        P.V("dve", "tensor_tensor", k1, k1, cs_, ALU.mult)
        P.V("pool", "tensor_tensor", k2, k2, sn_, ALU.mult)
        P.V("dve", "tensor_tensor", kb, k1, k2, ALU.add)
        P.load(krT_t, krT, kb, kb.ap)
        xb = [P.sb([128, 512], F32) for _ in range(2)]; sq = [P.sb([128, 512], F32) for _ in range(2)]
        ps = P.ps([128, 512]); rs = P.sb([128, 512], F32); ob = [P.sb([128, 512], BF16) for _ in range(2)]
        gcol = P.sb([128, 16], F32)
        for kc in range(12):
            P.load(gcol, gcol[:, kc:kc + 1], T(qn), qn[kc * 128:(kc + 1) * 128, :])
        for kc in range(4):
            P.load(gcol, gcol[:, 12 + kc:13 + kc], T(kvn), kvn[kc * 128:(kc + 1) * 128, :])
        for (r0, nk, gc0, dst_t, width) in ((0, 12, 0, cqn_t, 1536.0), (1536, 4, 12, ckn_t, 512.0)):
            for t0 in range(0, Tn, 512):
                ts_ = slice(t0, t0 + 512)
                for kc in range(nk):
                    x, s = xb[kc % 2], sq[kc % 2]
                    P.load(x, x.ap, cT_t, cT[r0 + kc * 128:r0 + (kc + 1) * 128, ts_], eng=("sp" if kc % 2 == 0 else "pool"))
                    P.V("act", "activation", s, x, AF.Square)
                    P.mm(ps, ps.ap, ones, ones.ap, s, s.ap, kc == 0, kc == nk - 1)
                P.V("act", "activation", rs, ps, AF.Sqrt, bias=(eps, eps.ap), scale=1.0 / width)
                P.V("dve", "reciprocal", rs, rs)
                for kc in range(nk):
                    x, o = xb[kc % 2], ob[kc % 2]
                    P.load(x, x.ap, cT_t, cT[r0 + kc * 128:r0 + (kc + 1) * 128, ts_], eng=("sp" if kc % 2 == 0 else "pool"))
                    P.V("dve", "scalar_tensor_tensor", o, x, (gcol, gcol[:, gc0 + kc:gc0 + kc + 1]), rs, ALU.mult, ALU.mult)
                    P.load(dst_t, dst_t.ap[kc * 128:(kc + 1) * 128, ts_], o, o.ap, eng="pool")
        P.barrier(); P.es = old
    qraw = dram(nc, "a_qraw", [nh, 128, Tn], F32); qraw_t = T(qraw)
    qnT = dram(nc, "a_qnT", [nh, 128, Tn], BF16); qnT_t = T(qnT)
    qrT = dram(nc, "a_qrT", [nh, 64, Tn], BF16); qrT_t = T(qrT)
    knT = dram(nc, "a_knT", [nh, 128, Tn], BF16); knT_t = T(knT)
    vT = dram(nc, "a_vT", [nh, 128, Tn], BF16); vT_t = T(vT)
    for h in range(nh):
        linear_fm(P, cqn_t, cqn, 1536, Tn, T(Wq), Wq[h], 256, [(0, 128, qnT_t, qnT[h], BF16, {}), (128, 256, qraw_t, qraw[h], F32, {})])
        linear_fm(P, ckn_t, ckn, 512, Tn, T(Wkv), Wkv[h], 256, [(0, 128, knT_t, knT[h], BF16, {}), (128, 256, vT_t, vT[h], BF16, {})])
    yoT = yoT_t.ap
    scale = 192 ** -0.5
    with ExitStack() as es2:
        old = P.es; P.es = es2
        cs_ = P.sb([64, Tn], F32); sn_ = P.sb([64, Tn], F32)
        P.load(cs_, cs_.ap, tab_t, tab[0]); P.load(sn_, sn_.ap, tab_t, tab[1], eng="pool")
        kr = P.sb([64, Tn], BF16); P.load(kr, kr.ap, krT_t, krT)
        onesb = P.sb([128, 128], BF16); idb = P.sb([128, 128], BF16)
        P.load(idb, idb.ap, T(consts['ident128']), consts['ident128'])
        P.V("dve", "memset", onesb, 1.0)
        q1 = P.sb([64, Tn], F32); q2 = P.sb([64, Tn], F32)
        Qn = P.sb([128, Tn], BF16); Qr = P.sb([64, Tn], BF16); Kn = P.sb([128, Tn], BF16); Vf = P.sb([128, Tn], BF16)
        Vt = P.sb([128, Tn // 128, 128], BF16)
        psS = [P.ps([128, 512]) for _ in range(2)]; psO = P.ps([128, 512]); psD = P.ps([128, 512]); psV = P.ps([128, 4, 128])
        pt = [P.sb([128, 512], BF16) for _ in range(2)]
        rec = P.sb([128, 512], F32); ob = [P.sb([128, 512], BF16) for _ in range(2)]
        for h in range(nh):
            P.load(q1, q1.ap, qraw_t, qraw[h, 0:64, :]); P.load(q2, q2.ap, qraw_t, qraw[h, 64:128, :], eng="pool")
            P.V("dve", "tensor_tensor", q1, q1, cs_, ALU.mult)
            P.V("pool", "tensor_tensor", q2, q2, sn_, ALU.mult)
            P.V("dve", "tensor_tensor", Qr, q1, q2, ALU.add)
            P.load(Qn, Qn.ap, qnT_t, qnT[h]); P.load(Kn, Kn.ap, knT_t, knT[h], eng="pool"); P.load(Vf, Vf.ap, vT_t, vT[h])
            for c0 in range(0, Tn // 128, 4):
                for i in range(4):
                    P.mm(psV, psV[:, i, :], Vf, Vf[:, (c0 + i) * 128:(c0 + i + 1) * 128], idb, idb.ap, True, True)
                P.V("act", "copy", (Vt, Vt[:, c0:c0 + 4, :]), psV)
            for qb in range(Tn // 512):
                qs = slice(qb * 512, (qb + 1) * 512)
                nkt = Tn // 128
                for kt in range(nkt):
                    ks = slice(kt * 128, (kt + 1) * 128)
                    s_ = psS[kt % 2]; p_ = pt[kt % 2]
                    P.mm(s_, s_.ap, Kn, Kn[:, ks], Qn, Qn[:, qs], True, False)
                    P.mm(s_, s_.ap, kr, kr[:, ks], Qr, Qr[:, qs], False, True)
                    P.V("act", "activation", p_, s_, AF.Exp, scale=scale)
                    P.mm(psO, psO.ap, Vt, Vt[:, kt, :], p_, p_.ap, kt == 0, kt == nkt - 1)
                    P.mm(psD, psD.ap, onesb, onesb.ap, p_, p_.ap, kt == 0, kt == nkt - 1)
                o = ob[qb % 2]
                P.V("dve", "reciprocal", rec, psD)
                P.V("dve", "tensor_tensor", o, psO, rec, ALU.mult)
                P.load(yoT_t, yoT[yo_row0 + h * 128:yo_row0 + (h + 1) * 128, qs], o, o.ap, eng="pool")
        P.barrier(); P.es = old

bfnp = ml_dtypes.bfloat16
_PROGS = {}


def _build(key, ins, outs, body):
    if key in _PROGS:
        return _PROGS[key]
    nc = bass.Bass("TRN2", target_bir_lowering=False)
    aps = {}
    for n, arr in ins.items():
        dtp = BF16 if arr.dtype == bfnp else (I32 if arr.dtype == np.int32 else F32)
        aps[n] = nc.dram_tensor(n, list(arr.shape), dtp, kind="ExternalInput").ap()
    oaps = {n: nc.dram_tensor(n, list(sh), dtp, kind="ExternalOutput").ap() for n, (sh, dtp) in outs.items()}
    with ExitStack() as es:
        P = Prog(nc, es)
        ots = {n: T(a) for n, a in oaps.items()}
        body(P, aps, ots)
        P.wait_all("sp", list(ots.values()))
        P.emit()
    _PROGS[key] = nc
    return nc


def _run(key, in_maps, outs, body):
    nc = _build(key, in_maps[0], outs, body)
    res = run_bass_kernel_spmd(nc, in_maps, core_ids=list(range(len(in_maps))))
    return res.results


def _col(v):
    return np.ascontiguousarray(np.asarray(v, np.float32).reshape(-1, 1))


def p0_body(P, a, o):
    cT, w, b = a['cT'], a['w'], a['b']
    o_t = o['o']; oo = o_t.ap
    cT_d, w_d, b_d = T(cT), T(w), T(b)
    ct = P.sb([128, 64], F32); cs = P.sb([128, 64], F32); ones = P.sb([1, 4], F32)
    bias = P.sb([1, 6144], F32); osb = P.sb([4, 6144], F32)
    wb = [P.sb([128, 2048], F32) for _ in range(4)]
    pst = [P.ps([4, 512], F32) for _ in range(8)]
    P.load(ct, ct[:, :], cT_d, cT[:, :]); P.load(bias, bias[:, :], b_d, b[:, :])
    P.V("act", "activation", cs, ct, AF.Silu)
    P.V("dve", "memset", ones, 1.0)
    i = 0
    for ng in range(3):
        pss = [pst[(ng % 2) * 4 + j] for j in range(4)]
        for j in range(4):
            c0 = ng * 2048 + j * 512
            P.mm(pss[j], pss[j][:, :], ones, ones[:, :], bias, bias[0:1, c0:c0 + 512], True, False)
        for kc in range(16):
            buf = wb[i % 4]; i += 1
            P.load(buf, buf[:, :], w_d, w[kc * 128:(kc + 1) * 128, ng * 2048:(ng + 1) * 2048], eng=("sp" if kc % 2 == 0 else "pool"))
            for j in range(4):
                P.mm(pss[j], pss[j][:, :], cs, cs[:, kc * 4:(kc + 1) * 4], buf, buf[:, j * 512:(j + 1) * 512], False, kc == 15)
        for j in range(4):
            c0 = ng * 2048 + j * 512
            P.V("dve", "tensor_copy", (osb, osb[:, c0:c0 + 512]), pss[j])
    P.load(o_t, oo[:, :], osb, osb[:, :])


D_, TN = 2048, 4096
CONSTS = None


def _consts():
    global CONSTS
    if CONSTS is None:
        CONSTS = make_consts()
        CONSTS['ones16'] = np.ones((16, 16), np.float32)
        inv = (10000.0 ** (-np.arange(0, 64, 2, dtype=np.float32) / 64)).astype(np.float32)
        rc = np.zeros((64, 3), np.float32)
        rc[:, 0] = np.concatenate([inv, inv]); rc[:32, 1] = -1; rc[32:, 1] = 1; rc[:, 2] = -PI
        CONSTS['ropec'] = rc
    return CONSTS


def even_body(P, a, o):
    nc = P.nc
    cst = {k[2:]: v for k, v in a.items() if k.startswith('c_')}
    hT_t = T(dram(nc, "e_hT", [D_, TN], BF16))
    mod_phase(P, D_, TN, T(a['xT']), a['sc'], a['sh'], hT_t)
    yoT_t = T(dram(nc, "e_yoT", [2048, TN], BF16))
    mla_phase(P, D_, TN, 8, hT_t, a['Wc'], a['qn'], a['kvn'], a['Wq'], a['Wkv'], a['pos'], a['c_ropec'], cst, yoT_t, 0)
    rwkv_full(P, D_, TN, 8, hT_t, a['Wr'], a['mu'], a['W2'], a['w0'], a['A2'], a['a0'], a['G2'], a['kk'], a['ka'], a['rk'], a['gng'], a['gnb'],
              cst, yoT_t, 1024)
    linear_fm(P, yoT_t, yoT_t.ap, 2048, TN, T(a['Wout']), a['Wout'], D_, [(0, D_, o['yT'], o['yT'].ap, F32, {})])


def odd_body(P, a, o):
    nc = P.nc
    cst = {k[2:]: v for k, v in a.items() if k.startswith('c_')}
    hT_t = T(dram(nc, "o_hT", [D_, TN], BF16))
    mod_phase(P, D_, TN, T(a['xT']), a['sc'], a['sh'], hT_t)
    gdn_layer(P, D_, TN, 8, hT_t.ap, a['W'], a['convw'], a['alog'], a['dtb'], a['normg'], a['Wout'], cst, o['yT'].ap, o['yT'])


def lb_body(P, a, o):
    nc = P.nc
    ln_phase(P, D_, TN, T(a['xT']), T(a['ya']), T(a['yb']), a['gate'], a['lng'], a['lnb'], a['c_ones128'], o['x1T'])
    hT_t = T(dram(nc, "b_hT", [D_, TN], BF16))
    mod_phase(P, D_, TN, o['x1T'], a['sc'], a['sh'], hT_t)
    moe_phase(P, D_, TN, hT_t, a['Wr'], a['Wg'], a['Wu'], a['Wd'], a['c_ones16'], 8, 0, o['ymT'])


def lc_body(P, a, o):
    ln_phase(P, D_, TN // 2, T(a['xT']), T(a['ya']), T(a['yb']), a['gate'], a['lng'], a['lnb'], a['c_ones128'], o['oT'])


def kernel(**inp):
    g = lambda k: np.asarray(inp[k])
    cst = _consts()
    cin = {"c_" + k: v for k, v in cst.items()}
    x = g('x'); B = 4
    c = g('c'); ada_w = g('ada_w'); ada_b = g('ada_b')
    cTh = np.ascontiguousarray(c.T.reshape(16, 128, 4).transpose(1, 0, 2).reshape(128, 64))
    maps = []
    for k in range(8):
        l, hf = k // 2, k % 2
        maps.append(dict(cT=cTh, w=np.ascontiguousarray(ada_w[l][:, hf * 6144:(hf + 1) * 6144]),
                         b=np.ascontiguousarray(ada_b[l][None, hf * 6144:(hf + 1) * 6144])))
    r = _run('p0', maps, dict(o=([4, 6144], F32)), p0_body)
    mod = np.zeros((4, 4, 12288), np.float32)
    for k in range(8):
        mod[k // 2][:, (k % 2) * 6144:(k % 2 + 1) * 6144] = r[k]['o']
    xT = [np.ascontiguousarray(x[b].T) for b in range(B)]
    pos = g('positions').astype(np.int32)
    ccommon = {k: cin[k] for k in ('c_bones', 'c_ident2', 'c_ident128', 'c_ones128', 'c_mA0', 'c_mB0', 'c_mC0', 'c_mA1', 'c_mB1', 'c_mC1')}
    for i in range(4):
        sh_m, sc_m, g_m, sh_f, sc_f, g_f = np.split(mod[i], 6, axis=-1)
        j = i // 2
        maps = []
        if i % 2 == 0:
            w_in = g('e_w_in')[j]; mu = g('e_shift_mu')[j]; w_uq = g('mla_w_uq')[j]; w_ukv = g('mla_w_ukv')[j]
            wout = g('e_w_out')[j]
            for k in range(8):
                b, hh = k // 2, k % 2
                Wc = np.concatenate([w_in[:, :2112], w_in[:, 2080:2112], w_in[:, 2048:2080]], 1)
                Wq = np.stack([np.concatenate([w_uq[:, hg * 192:hg * 192 + 192], w_uq[:, hg * 192 + 160:hg * 192 + 192],
                                               w_uq[:, hg * 192 + 128:hg * 192 + 160]], 1) for hg in range(hh * 8, hh * 8 + 8)])
                Wkv = np.stack([w_ukv[:, hg * 256:(hg + 1) * 256] for hg in range(hh * 8, hh * 8 + 8)])
                o0 = 2112; fs = slice(hh * 1024, hh * 1024 + 1024)
                idx = np.concatenate([np.arange(hh * 1024, hh * 1024 + 1024), 2048 + np.arange(hh * 1024, hh * 1024 + 1024),
                                      4096 + np.arange(hh * 1024, hh * 1024 + 1024), np.arange(6144, 6784)])
                maps.append(dict(xT=xT[b], sc=_col(sc_m[b]), sh=_col(sh_m[b]), Wc=np.ascontiguousarray(Wc), qn=_col(g('mla_q_norm')[j]),
                                 kvn=_col(g('mla_kv_norm')[j]), Wq=np.ascontiguousarray(Wq), Wkv=np.ascontiguousarray(Wkv),
                                 pos=np.ascontiguousarray(pos[b][None, :]),
                                 Wr=np.ascontiguousarray(w_in[:, o0 + idx]), mu=_col(mu[idx]),
                                 W2=np.ascontiguousarray(g('rwkv_w2')[j][:, :, fs]), w0=np.ascontiguousarray(g('rwkv_w0')[j][:, fs, None]),
                                 A2=np.ascontiguousarray(g('rwkv_a2')[j][:, :, fs]), a0=np.ascontiguousarray(g('rwkv_a0')[j][:, fs, None]),
                                 G2=np.ascontiguousarray(g('rwkv_g2')[j][:, fs]), kk=_col(g('rwkv_k_k')[j][fs]), ka=_col(g('rwkv_k_a')[j][fs]),
                                 rk=_col(g('rwkv_r_k')[j].reshape(-1)[fs]), gng=_col(g('rwkv_gn_g')[j][fs]), gnb=_col(g('rwkv_gn_b')[j][fs]),
                                 Wout=np.ascontiguousarray(np.concatenate([wout[hh * 1024:hh * 1024 + 1024], wout[2048 + hh * 1024:2048 + hh * 1024 + 1024]], 0)),
                                 c_ropec=cin['c_ropec'], **ccommon))
            r = _run('even', maps, dict(yT=([D_, TN], F32)), even_body)
        else:
            w_in = g('o_w_in')[j]; conv = g('gdn_conv')[j]
            for k in range(8):
                b, hh = k // 2, k % 2
                idx = np.concatenate([np.arange(hh * 1024, hh * 1024 + 1024), 2048 + np.arange(hh * 1024, hh * 1024 + 1024),
                                      4096 + np.arange(hh * 2048, hh * 2048 + 2048)])
                hsel = np.concatenate([np.arange(hh * 16, hh * 16 + 16), 32 + np.arange(hh * 16, hh * 16 + 16)])
                cols = np.concatenate([idx, 8192 + np.arange(hh * 2048, hh * 2048 + 2048), 12288 + hsel, 12352 + hsel])
                maps.append(dict(xT=xT[b], sc=_col(sc_m[b]), sh=_col(sh_m[b]), W=np.ascontiguousarray(w_in[:, cols]),
                                 convw=np.ascontiguousarray(conv[:, idx].T), alog=_col(g('gdn_a_log')[j][:, hh * 16:hh * 16 + 16]),
                                 dtb=_col(g('gdn_dt_bias')[j][:, hh * 16:hh * 16 + 16]), normg=_col(g('gdn_norm')[j]),
                                 Wout=np.ascontiguousarray(g('o_w_out')[j][hh * 2048:(hh + 1) * 2048]), **ccommon))
            r = _run('odd', maps, dict(yT=([D_, TN], F32)), odd_body)
        ys = [r[k]['yT'] for k in range(8)]
        maps = []
        wr = g('moe_router')[i]
        for k in range(8):
            b, hh = k // 2, k % 2
            perm = np.concatenate([np.arange(hh * 8, hh * 8 + 8), np.arange((1 - hh) * 8, (1 - hh) * 8 + 8)])
            maps.append(dict(xT=xT[b], ya=ys[2 * b], yb=ys[2 * b + 1], gate=_col(g_m[b]), lng=_col(g('ln_g')[i, 0]), lnb=_col(g('ln_b')[i, 0]),
                             sc=_col(sc_f[b]), sh=_col(sh_f[b]), Wr=np.ascontiguousarray(wr[:, perm]),
                             Wg=np.ascontiguousarray(g('moe_w_gate')[i][hh * 8:hh * 8 + 8]), Wu=np.ascontiguousarray(g('moe_w_up')[i][hh * 8:hh * 8 + 8]),
                             Wd=np.ascontiguousarray(g('moe_w_down')[i][hh * 8:hh * 8 + 8].reshape(8192, 2048)),
                             c_ones128=cin['c_ones128'], c_ones16=cin['c_ones16']))
        r = _run('lb', maps, dict(x1T=([D_, TN], F32), ymT=([D_, TN], F32)), lb_body)
        x1T = [r[2 * b]['x1T'] for b in range(B)]
        yms = [r[k]['ymT'] for k in range(8)]
        del r
        maps = []
        for k in range(8):
            b, hh = k // 2, k % 2
            ts_ = slice(hh * 2048, (hh + 1) * 2048)
            maps.append(dict(xT=np.ascontiguousarray(x1T[b][:, ts_]), ya=np.ascontiguousarray(yms[2 * b][:, ts_]),
                             yb=np.ascontiguousarray(yms[2 * b + 1][:, ts_]), gate=_col(g_f[b]), lng=_col(g('ln_g')[i, 1]), lnb=_col(g('ln_b')[i, 1]),
                             c_ones128=cin['c_ones128']))
        r = _run('lc', maps, dict(oT=([D_, TN // 2], F32)), lc_body)
        xT = [np.ascontiguousarray(np.concatenate([r[2 * b]['oT'], r[2 * b + 1]['oT']], 1)) for b in range(B)]
    return np.ascontiguousarray(np.stack([xT[b].T for b in range(B)])).astype(np.float32)
```
